# Optimizing a Trainium2 kernel written in Bass

```python
import math
import jax, jax.numpy as jnp
from jax import lax
import numpy as np

D_MODEL = 1024
BATCH = 8
SEQ = 8192
DEPTH = 1
DEC_BATCH = 16
DEC_SEQ = 32
PAST_LEN = 1024

CHUNK = 64
MIX_WIDTH = D_MODEL
GDN_HEADS = 4
GDN_DK = MIX_WIDTH // 2 // GDN_HEADS
GDN_DV = MIX_WIDTH // 2 // GDN_HEADS
GDN_KEY_DIM = GDN_HEADS * GDN_DK
GDN_VAL_DIM = GDN_HEADS * GDN_DV
GDN_CONV_DIM = 2 * GDN_KEY_DIM + GDN_VAL_DIM
CONV_W = 4
DIFF_HEADS = 4
DIFF_DV = MIX_WIDTH // 2 // DIFF_HEADS
DIFF_DQK = DIFF_DV // 2
ROT_DIM = DIFF_DQK // 4
ROPE_THETA = 500000.0
Q_BLOCK = 128
N_MEM = 256
MEM_HEADS = 4
MEM_HD = 128
D_FF = 4 * D_MODEL
NORM_EPS = 1e-6
_IN_SIZES = (GDN_CONV_DIM, GDN_VAL_DIM, GDN_HEADS, GDN_HEADS,
             DIFF_HEADS * 2 * DIFF_DQK, DIFF_HEADS * 2 * DIFF_DQK, DIFF_HEADS * DIFF_DV)
D_IN = sum(_IN_SIZES)
IN_SPLITS = tuple(int(s) for s in np.cumsum(_IN_SIZES)[:-1])

kernel_name = 'hybrid_gdn_diffattn_stream_step'


def rms_norm(x, g):
    xf = x.astype(jnp.float32)
    y = xf * lax.rsqrt(jnp.mean(xf * xf, axis=-1, keepdims=True) + NORM_EPS)
    return (y * g.astype(jnp.float32)).astype(x.dtype)


def l2_normalize(x):
    xf = x.astype(jnp.float32)
    return xf * lax.rsqrt(jnp.sum(xf * xf, axis=-1, keepdims=True) + NORM_EPS)


def apply_partial_rope(x, pos):
    inv = ROPE_THETA ** (-jnp.arange(0, ROT_DIM, 2, dtype=jnp.float32) / ROT_DIM)
    ang = pos.astype(jnp.float32)[:, None] * inv[None, :]
    cos = jnp.cos(ang)[None, :, None, None, :]
    sin = jnp.sin(ang)[None, :, None, None, :]
    xr = x[..., :ROT_DIM].astype(jnp.float32)
    x1, x2 = xr[..., :ROT_DIM // 2], xr[..., ROT_DIM // 2:]
    rot = jnp.concatenate([x1 * cos - x2 * sin, x2 * cos + x1 * sin], axis=-1)
    return jnp.concatenate([rot.astype(x.dtype), x[..., ROT_DIM:]], axis=-1)


def causal_conv(x, prev, w):
    t = x.shape[1]
    xp = jnp.concatenate([prev.astype(x.dtype), x], axis=1)
    y = xp[:, 0:t] * w[0]
    for i in range(1, CONV_W):
        y = y + xp[:, i:i + t] * w[i]
    return y, xp[:, -(CONV_W - 1):]


def gated_delta_rule(q, k, v, g, beta, s0):
    b, t, h, _ = q.shape
    dv = v.shape[-1]
    c = min(t, CHUNK)
    n = t // c
    f32 = jnp.float32

    def blocks(a):
        return a.astype(f32).reshape((b, n, c) + a.shape[2:]).swapaxes(2, 3)

    qb, kb, vb, gb, bb = (blocks(a) for a in (q, k, v, g, beta))
    gc = jnp.cumsum(gb, axis=-1)
    idx = jnp.arange(c)
    tril = idx[:, None] >= idx[None, :]
    strict = idx[:, None] > idx[None, :]
    decay = jnp.exp(jnp.where(tril, gc[..., :, None] - gc[..., None, :], -jnp.inf))
    kbeta = kb * bb[..., None]
    m = jnp.where(strict, jnp.einsum('bnhid,bnhjd->bnhij', kbeta, kb) * decay, 0.0)
    eye = jnp.eye(c, dtype=f32)
    tinv = lax.linalg.triangular_solve(eye + m, jnp.broadcast_to(eye, m.shape), left_side=True, lower=True)
    u = tinv @ (vb * bb[..., None])
    w = tinv @ (kbeta * jnp.exp(gc)[..., None])
    aqk = jnp.einsum('bnhid,bnhjd->bnhij', qb, kb) * decay
    qg = qb * jnp.exp(gc)[..., None]
    kd = kb * jnp.exp(gc[..., -1:] - gc)[..., None]
    glast = jnp.exp(gc[..., -1])

    def step(s, xs):
        u_n, w_n, a_n, qg_n, kd_n, gl_n = xs
        v_new = u_n - jnp.einsum('bhck,bhkv->bhcv', w_n, s)
        o_n = jnp.einsum('bhck,bhkv->bhcv', qg_n, s) + jnp.einsum('bhij,bhjv->bhiv', a_n, v_new)
        s = s * gl_n[..., None, None] + jnp.einsum('bhck,bhcv->bhkv', kd_n, v_new)
        return s, o_n

    xs = tuple(jnp.moveaxis(a, 1, 0) for a in (u, w, aqk, qg, kd, glast))
    s_fin, o = lax.scan(step, s0.astype(f32), xs)
    o = jnp.moveaxis(o, 0, 1).swapaxes(2, 3).reshape(b, t, h, dv)
    return o, s_fin


def gdn_mixer(qkv, z, a, bgate, conv_prev, s0, conv_w, a_log, dt_bias, norm_g):
    b, t, _ = qkv.shape
    y, conv_new = causal_conv(qkv, conv_prev, conv_w)
    y = jax.nn.silu(y)
    q, k, v = jnp.split(y, [GDN_KEY_DIM, 2 * GDN_KEY_DIM], axis=-1)
    q = l2_normalize(q.reshape(b, t, GDN_HEADS, GDN_DK)) * (GDN_DK ** -0.5)
    k = l2_normalize(k.reshape(b, t, GDN_HEADS, GDN_DK))
    v = v.reshape(b, t, GDN_HEADS, GDN_DV)
    beta = jax.nn.sigmoid(bgate.astype(jnp.float32))
    g = -jnp.exp(a_log.astype(jnp.float32)) * jax.nn.softplus(a.astype(jnp.float32) + dt_bias.astype(jnp.float32))
    o, s_new = gated_delta_rule(q, k, v, g, beta, s0)
    o = rms_norm(o.astype(qkv.dtype), norm_g) * jax.nn.silu(z.reshape(b, t, GDN_HEADS, GDN_DV))
    return o.reshape(b, t, GDN_VAL_DIM), s_new, conv_new


def diff_attend(q, k, v, q_pos, k_pos, lam):
    s = jnp.einsum('bqhsd,bkhsd->bhsqk', q, k).astype(jnp.float32) * (DIFF_DQK ** -0.5)
    mask = (k_pos[None, :] // CHUNK) <= (q_pos[:, None] // CHUNK)
    s = jnp.where(mask, s, -jnp.inf)
    p = jax.nn.softmax(s, axis=-1)
    att = p[:, :, 0] - lam * p[:, :, 1]
    return jnp.einsum('bhqk,bkhd->bqhd', att.astype(v.dtype), v)


def diff_attention_blocked(q, k, v, pos, lam):
    b, t = q.shape[:2]
    nb = t // Q_BLOCK
    qb = q.reshape((b, nb, Q_BLOCK) + q.shape[2:]).swapaxes(0, 1)
    pb = pos.reshape(nb, Q_BLOCK)

    def one(args):
        q_blk, p_blk = args
        return diff_attend(q_blk, k, v, p_blk, pos, lam)

    o = lax.map(one, (qb, pb))
    return o.swapaxes(0, 1).reshape(b, t, DIFF_HEADS, DIFF_DV)


def cross_attend(h, mem_k, mem_v, w_mq, w_mo):
    b, t, _ = h.shape
    q = (h @ w_mq).reshape(b, t, MEM_HEADS, MEM_HD)
    s = jnp.einsum('bqhd,bkhd->bhqk', q, mem_k.astype(q.dtype)).astype(jnp.float32) * (MEM_HD ** -0.5)
    p = jax.nn.softmax(s, axis=-1)
    o = jnp.einsum('bhqk,bkhd->bqhd', p.astype(h.dtype), mem_v.astype(h.dtype))
    return o.reshape(b, t, MEM_HEADS * MEM_HD) @ w_mo


def encoder_layer(x, pos, conv_prev, s0, kv_prev, mem_k, mem_v, lw, layer_idx):
    (norm_mix_g, w_in, gdn_conv_w, gdn_a_log, gdn_dt_bias, gdn_norm_g, diff_lambda, diff_norm_g,
     w_out, norm_mem_g, w_mq, w_mo, norm_ffn_g, w_up, w_down) = lw
    b, t, _ = x.shape
    h = rms_norm(x, norm_mix_g)
    proj = h @ w_in
    qkv_g, z_g, a_g, b_g, q_d, k_d, v_d = jnp.split(proj, IN_SPLITS, axis=-1)
    if conv_prev is None:
        conv_prev = jnp.zeros((b, CONV_W - 1, GDN_CONV_DIM), proj.dtype)
    if s0 is None:
        s0 = jnp.zeros((b, GDN_HEADS, GDN_DK, GDN_DV), jnp.float32)
    o_gdn, s_new, conv_new = gdn_mixer(qkv_g, z_g, a_g, b_g, conv_prev, s0,
                                       gdn_conv_w, gdn_a_log, gdn_dt_bias, gdn_norm_g)
    q_d = apply_partial_rope(q_d.reshape(b, t, DIFF_HEADS, 2, DIFF_DQK), pos)
    k_d = apply_partial_rope(k_d.reshape(b, t, DIFF_HEADS, 2, DIFF_DQK), pos)
    v_d = v_d.reshape(b, t, DIFF_HEADS, DIFF_DV)
    lam_init = 0.8 - 0.6 * math.exp(-0.3 * layer_idx)
    lf = diff_lambda.astype(jnp.float32)
    lam = jnp.exp(jnp.sum(lf[0] * lf[1])) - jnp.exp(jnp.sum(lf[2] * lf[3])) + lam_init
    if kv_prev is None:
        o_diff = diff_attention_blocked(q_d, k_d, v_d, pos, lam)
    else:
        k_prev, v_prev = kv_prev
        p_len = k_prev.shape[1]
        k_all = jnp.concatenate([k_prev.reshape(b, p_len, DIFF_HEADS, 2, DIFF_DQK).astype(k_d.dtype), k_d], axis=1)
        v_all = jnp.concatenate([v_prev.astype(v_d.dtype), v_d], axis=1)
        o_diff = diff_attend(q_d, k_all, v_all, pos, jnp.arange(p_len + t), lam)
    o_diff = rms_norm(o_diff, diff_norm_g) * (1.0 - lam_init)
    mix = jnp.concatenate([o_gdn, o_diff.reshape(b, t, DIFF_HEADS * DIFF_DV)], axis=-1)
    x = x + mix @ w_out
    x = x + cross_attend(rms_norm(x, norm_mem_g), mem_k, mem_v, w_mq, w_mo)
    hf = rms_norm(x, norm_ffn_g)
    x = x + jnp.square(jax.nn.relu(hf @ w_up)) @ w_down
    return x, s_new, conv_new, k_d.reshape(b, t, DIFF_HEADS, 2 * DIFF_DQK), v_d


def setup_inputs(seed: int = 0) -> dict:
    key = jax.random.key(seed)
    ks = jax.random.split(key, 32)
    f32 = jnp.float32
    nrm = lambda k, shape, s: jax.random.normal(k, shape, f32) * s
    gain = lambda k, shape: 1.0 + 0.05 * jax.random.normal(k, shape, f32)
    dt = jnp.exp(jax.random.uniform(ks[16], (DEPTH, GDN_HEADS), f32, math.log(1e-3), math.log(1e-1)))
    return {
        'x_prompt': nrm(ks[0], (BATCH, SEQ, D_MODEL), 1.0),
        'x_sample': nrm(ks[1], (DEC_BATCH, DEC_SEQ, D_MODEL), 1.0),
        'mem_prompt': nrm(ks[2], (BATCH, N_MEM, D_MODEL), 1.0),
        'cache_diff_k': nrm(ks[3], (DEPTH, DEC_BATCH, PAST_LEN, DIFF_HEADS, 2 * DIFF_DQK), 1.0),
        'cache_diff_v': nrm(ks[4], (DEPTH, DEC_BATCH, PAST_LEN, DIFF_HEADS, DIFF_DV), 1.0),
        'cache_mem_k': nrm(ks[5], (DEPTH, DEC_BATCH, N_MEM, MEM_HEADS, MEM_HD), 1.0),
        'cache_mem_v': nrm(ks[6], (DEPTH, DEC_BATCH, N_MEM, MEM_HEADS, MEM_HD), 1.0),
        'state_gdn': nrm(ks[7], (DEPTH, DEC_BATCH, GDN_HEADS, GDN_DK, GDN_DV), 0.1),
        'state_gdn_conv': nrm(ks[8], (DEPTH, DEC_BATCH, CONV_W - 1, GDN_CONV_DIM), 1.0),
        'norm_mix_g': gain(ks[9], (DEPTH, D_MODEL)),
        'w_in': nrm(ks[10], (DEPTH, D_MODEL, D_IN), D_MODEL ** -0.5),
        'gdn_conv_w': nrm(ks[11], (DEPTH, CONV_W, GDN_CONV_DIM), CONV_W ** -0.5),
        'gdn_a_log': jnp.log(jax.random.uniform(ks[12], (DEPTH, GDN_HEADS), f32, 1.0, 16.0)),
        'gdn_dt_bias': jnp.log(jnp.expm1(dt)),
        'gdn_norm_g': gain(ks[13], (DEPTH, GDN_DV)),
        'diff_lambda': nrm(ks[14], (DEPTH, 4, DIFF_DQK), 0.1),
        'diff_norm_g': gain(ks[15], (DEPTH, DIFF_DV)),
        'w_out': nrm(ks[17], (DEPTH, MIX_WIDTH, D_MODEL), MIX_WIDTH ** -0.5),
        'norm_mem_g': gain(ks[18], (DEPTH, D_MODEL)),
        'mem_norm_g': gain(ks[19], (DEPTH, D_MODEL)),
        'w_mq': nrm(ks[20], (DEPTH, D_MODEL, MEM_HEADS * MEM_HD), D_MODEL ** -0.5),
        'w_mkv': nrm(ks[21], (DEPTH, D_MODEL, 2 * MEM_HEADS * MEM_HD), D_MODEL ** -0.5),
        'w_mo': nrm(ks[22], (DEPTH, MEM_HEADS * MEM_HD, D_MODEL), (MEM_HEADS * MEM_HD) ** -0.5),
        'norm_ffn_g': gain(ks[23], (DEPTH, D_MODEL)),
        'w_up': nrm(ks[24], (DEPTH, D_MODEL, D_FF), D_MODEL ** -0.5),
        'w_down': nrm(ks[25], (DEPTH, D_FF, D_MODEL), 0.5 * D_FF ** -0.5),
        'final_norm_g': gain(ks[26], (D_MODEL,)),
    }


def reference(x_prompt, x_sample, mem_prompt, cache_diff_k, cache_diff_v, cache_mem_k, cache_mem_v,
              state_gdn, state_gdn_conv, norm_mix_g, w_in, gdn_conv_w, gdn_a_log, gdn_dt_bias, gdn_norm_g,
              diff_lambda, diff_norm_g, w_out, norm_mem_g, mem_norm_g, w_mq, w_mkv, w_mo,
              norm_ffn_g, w_up, w_down, final_norm_g):
    bp, tp, _ = x_prompt.shape
    ts = x_sample.shape[1]
    p_len = cache_diff_k.shape[2]
    pos_p = jnp.arange(tp)
    pos_s = p_len + jnp.arange(ts)
    hp, hs = x_prompt, x_sample
    p_s, p_c, p_k, p_v, p_mk, p_mv = [], [], [], [], [], []
    s_s, s_c, s_k, s_v = [], [], [], []
    for l in range(DEPTH):
        lw = (norm_mix_g[l], w_in[l], gdn_conv_w[l], gdn_a_log[l], gdn_dt_bias[l], gdn_norm_g[l],
              diff_lambda[l], diff_norm_g[l], w_out[l], norm_mem_g[l], w_mq[l], w_mo[l],
              norm_ffn_g[l], w_up[l], w_down[l])
        mh = rms_norm(mem_prompt, mem_norm_g[l])
        mk, mv = jnp.split(mh @ w_mkv[l], 2, axis=-1)
        mk = mk.reshape(bp, -1, MEM_HEADS, MEM_HD)
        mv = mv.reshape(bp, -1, MEM_HEADS, MEM_HD)
        hp, sp, cp, kp, vp = encoder_layer(hp, pos_p, None, None, None, mk, mv, lw, l)
        hs, ss, cs, ks_, vs = encoder_layer(hs, pos_s, state_gdn_conv[l], state_gdn[l],
                                            (cache_diff_k[l], cache_diff_v[l]),
                                            cache_mem_k[l], cache_mem_v[l], lw, l)
        p_s.append(sp); p_c.append(cp); p_k.append(kp); p_v.append(vp); p_mk.append(mk); p_mv.append(mv)
        s_s.append(ss); s_c.append(cs); s_k.append(ks_); s_v.append(vs)
    y_prompt = rms_norm(hp, final_norm_g)
    y_sample = rms_norm(hs, final_norm_g)
    return (y_prompt, y_sample,
            jnp.stack(p_s), jnp.stack(p_c), jnp.stack(p_k), jnp.stack(p_v), jnp.stack(p_mk), jnp.stack(p_mv),
            jnp.stack(s_s), jnp.stack(s_c), jnp.stack(s_k), jnp.stack(s_v))
```

```python
import numpy as np
from contextlib import ExitStack
import concourse.bass as bass
import concourse.mybir as mybir
from concourse.bass_utils import run_bass_kernel_spmd

F32 = mybir.dt.float32
BF16 = mybir.dt.bfloat16
AF = mybir.ActivationFunctionType
ALU = mybir.AluOpType
AX = mybir.AxisListType

SAME_ENGINE_SYNC = True


class Track:
    __slots__ = ("name", "w", "re", "rd", "dsem", "dtotal", "excl", "dram")

    def __init__(self, name):
        self.name = name
        self.w = None
        self.re = {}
        self.rd = []
        self.dsem = None
        self.dtotal = 0
        self.excl = False
        self.dram = False


class Tile:
    def __init__(self, t, tr):
        self.t = t
        self.tr = tr

    def __getitem__(self, key):
        return self.t[key]


class Eng:
    def __init__(self, name, h, sem):
        self.name = name
        self.h = h
        self.sem = sem
        self.count = 0
        self.seen = {}


class KB:
    def __init__(self):
        self.nc = bass.Bass("TRN2", target_bir_lowering=False)
        self.es = ExitStack()
        nc = self.nc
        self.E = {}
        for name, h in (("pe", nc.tensor), ("act", nc.scalar), ("dve", nc.vector),
                        ("pool", nc.gpsimd), ("sp", nc.sync)):
            sem = self.es.enter_context(nc.semaphore("prog_" + name))
            self.E[name] = Eng(name, h, sem)
        self.nsem = 5
        self.sb_bytes = 0
        self.n_ins = 0
        self.ps_free = []
        self.all_out_tracks = []

    def sb(self, name, shape, dt):
        t = self.es.enter_context(self.nc.sbuf_tensor("sb_" + name, list(shape), dt))
        nb = int(np.prod(shape[1:])) * (4 if dt == F32 else 2)
        self.sb_bytes += nb
        return Tile(t, Track(name))

    def ps_init(self):
        self.ps_tiles = []
        for i in range(8):
            t = self.es.enter_context(self.nc.psum_tensor("psb%d" % i, [128, 512], F32))
            tl = Tile(t, Track("psb%d" % i))
            tl.tr.excl = True
            self.ps_tiles.append(tl)
        self.ps_free = list(self.ps_tiles)

    def psget(self):
        assert self.ps_free, "out of psum banks"
        return self.ps_free.pop(0)

    def psput(self, p):
        self.ps_free.append(p)

    def dram(self, name, shape, dt, kind="Internal"):
        t = self.nc.dram_tensor(name, list(shape), dt, kind=kind)
        tl = Tile(t.ap(), Track(name))
        tl.tr.dram = True
        return tl

    def newsem(self, name):
        self.nsem += 1
        return self.es.enter_context(self.nc.semaphore(name))

    def _wait(self, eng, ev):
        if ev is None:
            return
        if ev[0] == "e":
            src, idx = ev[1], ev[2]
            if src is eng and (not SAME_ENGINE_SYNC or eng.name in ("pe", "sp")):
                return
            assert src.count >= idx, "wait on unemitted inc %s %d>%d" % (src.name, idx, src.count)
            if eng.seen.get(src.name, 0) >= idx:
                return
            eng.h.wait_ge(src.sem, idx)
            eng.seen[src.name] = idx
            self.n_ins += 1
        else:
            tr = ev[1]
            val = tr.dtotal
            if eng.seen.get(id(tr), 0) >= val:
                return
            eng.h.wait_ge(tr.dsem, val)
            eng.seen[id(tr)] = val
            self.n_ins += 1

    def _pre(self, eng, r, w):
        for t in r:
            self._wait(eng, t.w)
            if t.excl:
                for en, idx in t.re.items():
                    if en != eng.name:
                        self._wait(eng, ("e", self.E[en], idx))
        for t in w:
            self._wait(eng, t.w)
            for en, idx in t.re.items():
                self._wait(eng, ("e", self.E[en], idx))
            for dtr in t.rd:
                self._wait(eng, ("d", dtr))

    def op(self, en, fn, r=(), w=(), inc=True):
        eng = self.E[en]
        r = [x.tr if isinstance(x, Tile) else x for x in r]
        w = [x.tr if isinstance(x, Tile) else x for x in w]
        self._pre(eng, r, w)
        ins = fn()
        self.n_ins += 1
        if inc:
            ins.then_inc(eng.sem, 1)
            eng.count += 1
            idx = eng.count
        else:
            idx = eng.count + 1
        ev = ("e", eng, idx)
        for t in w:
            t.w = ev
            t.re = {}
            t.rd = []
        for t in r:
            if t.re.get(en, 0) < idx:
                t.re[en] = idx
        return ins

    def dma(self, out_ap, in_ap, w, r=None, q="sp", **kw):
        eng = self.E[q]
        w = w.tr if isinstance(w, Tile) else w
        if r is not None:
            r = r.tr if isinstance(r, Tile) else r
        self._pre(eng, [r] if r is not None else [], [] if w.dram else [w])
        if w.dsem is None:
            w.dsem = self.newsem("d_" + w.name)
        ins = eng.h.dma_start(out=out_ap, in_=in_ap, **kw)
        ins.then_inc(w.dsem, 16)
        self.n_ins += 1
        w.dtotal += 16
        w.w = ("d", w)
        w.re = {}
        w.rd = []
        if r is not None:
            if w not in r.rd:
                r.rd.append(w)
        return ins

    def finish(self):
        eng = self.E["sp"]
        for tr in self.all_out_tracks:
            if tr.dsem is not None:
                self._wait(eng, ("d", tr))

    def mm(self, out, lhsT, rhs, r, w, start=True, stop=True, inc=None):
        if inc is None:
            inc = stop
        nc = self.nc
        return self.op("pe", lambda: nc.tensor.matmul(out, lhsT, rhs, start=start, stop=stop), r, w, inc)

    def tr_pe(self, out, in_, ident, r, w, inc=True):
        nc = self.nc
        return self.op("pe", lambda: nc.tensor.transpose(out, in_, ident), r, w, inc)

    def act(self, out, in_, func, r, w, scale=1.0, bias=None, accum=None):
        nc = self.nc
        kw = {}
        if bias is not None:
            kw["bias"] = bias
        if accum is not None:
            kw["accum_out"] = accum
        return self.op("act", lambda: nc.scalar.activation(out, in_, func, scale=scale, **kw), r, w)

    def _ve(self, en):
        return self.nc.vector if en == "dve" else self.nc.gpsimd

    def tt(self, en, out, in0, in1, op, r, w):
        e = self._ve(en)
        return self.op(en, lambda: e.tensor_tensor(out, in0, in1, op), r, w)

    def ts(self, en, out, in0, s1, op0, r, w, s2=None, op1=None):
        e = self._ve(en)
        if op1 is None:
            return self.op(en, lambda: e.tensor_scalar(out, in0, s1, None, op0), r, w)
        return self.op(en, lambda: e.tensor_scalar(out, in0, s1, s2, op0, op1), r, w)

    def stt(self, out, in0, scalar, in1, op0, op1, r, w):
        nc = self.nc
        return self.op("dve", lambda: nc.vector.scalar_tensor_tensor(out, in0, scalar, in1, op0, op1), r, w)

    def copy(self, en, out, in_, r, w):
        if en == "act":
            nc = self.nc
            return self.op("act", lambda: nc.scalar.copy(out, in_), r, w)
        e = self._ve(en)
        return self.op(en, lambda: e.tensor_copy(out, in_), r, w)

    def memset(self, en, ap, val, w):
        e = self._ve(en)
        return self.op(en, lambda: e.memset(ap, val), (), w)

    def recip(self, out, in_, r, w):
        nc = self.nc
        return self.op("dve", lambda: nc.vector.reciprocal(out, in_), r, w)

P = 128
DM = 1024
NCHUNK = 29
NSTREAM = 27
EPS = 1e-6
NEG = -1.0e5
PAST = 1024
TS_LEN = 32


def build(TP, NS=2, depth_pref=2, stage=99):
    K = KB()
    nc = K.nc
    K.ps_init()
    EI = "ExternalInput"
    EO = "ExternalOutput"
    d_xp = K.dram("x_prompt", [TP, DM], F32, EI)
    d_xs = K.dram("x_sample", [NS, TS_LEN, DM], F32, EI)
    d_mem = K.dram("mem_prompt", [256, DM], F32, EI)
    d_cdk = K.dram("cache_diff_k", [NS, PAST, 512], F32, EI)
    d_cdv = K.dram("cache_diff_v", [NS, PAST, 512], F32, EI)
    d_cmk = K.dram("cache_mem_k", [NS, 256, 512], F32, EI)
    d_cmv = K.dram("cache_mem_v", [NS, 256, 512], F32, EI)
    d_sg = K.dram("state_gdn", [NS, 4, 128, 128], F32, EI)
    d_sgc = K.dram("state_gdn_conv", [NS, 3, 1536], F32, EI)
    d_wch = K.dram("wch", [NCHUNK, 128, 4096], F32, EI)
    d_wab = K.dram("wab", [128, 64], F32, EI)
    d_gcols = K.dram("gcols", [128, 32], F32, EI)
    d_gfin = K.dram("gfin", [128, DM], F32, EI)
    d_convw = K.dram("convw", [128, 48], F32, EI)
    d_small = K.dram("smallp", [128, 8], F32, EI)
    d_lam = K.dram("lamp", [1, 256], F32, EI)
    d_ident = K.dram("ident", [128, 128], F32, EI)
    d_maskneg = K.dram("maskneg", [128, 128], F32, EI)
    d_noti = K.dram("noti", [128, 128], F32, EI)
    d_sel = K.dram("sel", [4, 512], F32, EI)
    d_ropeP = K.dram("ropeP", [TP, 256], F32, EI)
    d_ropeS = K.dram("ropeS", [TS_LEN, 256], F32, EI)

    o_yp = K.dram("y_prompt", [TP, DM], F32, EO)
    o_ys = K.dram("y_sample", [NS, TS_LEN, DM], F32, EO)
    o_psg = K.dram("p_state_gdn", [4, 128, 128], F32, EO)
    o_psc = K.dram("p_state_gdn_conv", [3, 1536], F32, EO)
    o_pdk = K.dram("p_diff_k", [TP, 512], F32, EO)
    o_pdv = K.dram("p_diff_v", [TP, 512], F32, EO)
    o_pmk = K.dram("p_mem_k", [256, 512], F32, EO)
    o_pmv = K.dram("p_mem_v", [256, 512], F32, EO)
    o_ssg = K.dram("s_state_gdn", [NS, 4, 128, 128], F32, EO)
    o_ssc = K.dram("s_state_gdn_conv", [NS, 3, 1536], F32, EO)
    o_sdk = K.dram("s_diff_k", [NS, TS_LEN, 512], F32, EO)
    o_sdv = K.dram("s_diff_v", [NS, TS_LEN, 512], F32, EO)
    for o in (o_yp, o_ys, o_psg, o_psc, o_pdk, o_pdv, o_pmk, o_pmv, o_ssg, o_ssc, o_sdk, o_sdv):
        K.all_out_tracks.append(o.tr)

    s_wbf = K.dram("s_wbf", [NCHUNK, 128, 4096], BF16)
    s_kTp = K.dram("s_kTp", [4, 128, TP], BF16)
    s_vp = K.dram("s_vp", [TP, 512], BF16)
    s_kTs = [K.dram("s_kTs%d" % i, [4, 128, PAST + TS_LEN], BF16) for i in range(NS)]
    s_vs = [K.dram("s_vs%d" % i, [PAST + TS_LEN, 512], BF16) for i in range(NS)]

    WMAX = 256
    ident_f = K.sb("ident_f", [128, 128], F32)
    ident_b = K.sb("ident_b", [128, 128], BF16)
    maskneg = K.sb("maskneg", [128, 128], F32)
    noti = K.sb("noti", [128, 128], F32)
    ones_b = K.sb("ones_b", [128, 128], BF16)
    onesm_b = K.sb("onesm_b", [128, 128], BF16)
    ones_f = K.sb("ones_f", [4, 128], F32)
    sel = K.sb("sel", [4, 512], F32)
    wab_f = K.sb("wab_f", [128, 64], F32)
    wab = K.sb("wab", [128, 8, 8], BF16)
    gcols = K.sb("gcols", [128, 4, 8], F32)
    G = [K.sb("G%d" % i, [128, 8, 128], BF16) for i in range(4)]
    gfin = K.sb("gfin", [128, DM], F32)
    convw = K.sb("convw", [128, 12, 4], F32)
    diagw = K.sb("diagw", [128, 48, 128], BF16)
    smallp = K.sb("smallp", [128, 8], F32)
    derived = K.sb("derived", [128, 8], F32)
    lamt = K.sb("lamt", [1, 256], F32)
    lamw = K.sb("lamw", [1, 16], F32)

    NSLOT = 2
    wring = [K.sb("wring%d" % i, [128, 4096], BF16) for i in range(NSLOT)]

    xin2 = [[K.sb("xin%d_%d" % (j, i), [128, DM], F32) for i in range(2)] for j in range(2)]
    xin = xin2[0]
    xn_b = [K.sb("xn_b%d" % i, [128, DM], BF16) for i in range(1)] * 2
    nrm = [K.sb("nrm%d" % i, [128, 4], F32) for i in range(2)]
    hT = K.sb("hT", [128, 8, WMAX], BF16)

    xpre = K.sb("xpre", [128, 12, 3 + WMAX], BF16)
    convst = K.sb("convst", [128, 12, 3], F32)
    zs = K.sb("zs", [128, 4, WMAX], BF16)
    knT = K.sb("knT", [128, 4, WMAX], BF16)
    qnT = K.sb("qnT", [128, 4, WMAX], BF16)
    qgT = K.sb("qgT", [128, 4, WMAX], BF16)
    vT = K.sb("vT", [128, 4, WMAX], BF16)
    oT = K.sb("oT", [128, 4, WMAX], F32)
    GCrow = K.sb("GCrow", [128, 4, WMAX], F32)
    EGrow = K.sb("EGrow", [128, 4, WMAX], F32)
    BROW = K.sb("BROW", [128, 4, WMAX], BF16)
    abT = K.sb("abT", [4, 3, WMAX], F32)
    cols = [K.sb("cols%d" % i, [128, 24], F32) for i in range(2)]
    ytmp = [K.sb("ytmp%d" % i, [128, WMAX], F32) for i in range(2)]
    sqtmp = [K.sb("sqtmp%d" % i, [128, WMAX], BF16) for i in range(1)]
    rn = [K.sb("rn%d" % i, [128, WMAX], F32) for i in range(1)]
    rinv = [K.sb("rinv%d" % i, [128, WMAX], F32) for i in range(1)]
    S_f = [K.sb("S_f%d" % h, [128, 128], F32) for h in range(4)]
    S_b = [K.sb("S_b%d" % h, [128, 128], BF16) for h in range(4)]
    TH = [(t, h) for t in range(2) for h in range(4)]
    kd = {th: K.sb("kd%d%d" % th, [128, 128], BF16) for th in TH}
    kbg = {th: K.sb("kbg%d%d" % th, [128, 128], F32) for th in TH}
    vb = {th: K.sb("vb%d%d" % th, [128, 128], F32) for th in TH}
    aqkT = {th: K.sb("aqkT%d%d" % th, [128, 128], BF16) for th in TH}
    Qm = {th: [K.sb("Q%d%d_%d" % (th + (i,)), [128, 128], F32) for i in range(2)] for th in TH}
    Rm = {th: [K.sb("R%d%d_%d" % (th + (i,)), [128, 128], F32) for i in range(2)] for th in TH}
    TTm = {th: K.sb("TT%d%d" % th, [128, 128], F32) for th in TH}
    gtmp = None
    nwT = [K.sb("nwT%d" % i, [128, 128], F32) for i in range(2)]
    vnew = [K.sb("vnew%d" % i, [128, 128], BF16) for i in range(2)]

    qk_st = [K.sb("qk_st%d" % i, [128, 1024], F32) for i in range(2)]
    yout = qk_st
    memst = qk_st
    v_st = [K.sb("v_st%d" % i, [128, 512], F32) for i in range(2)]
    cs = [K.sb("cs%d" % i, [128, 256], F32) for i in range(1)] * 2
    qk_b = [K.sb("qk_b%d" % i, [128, 1024], BF16) for i in range(1)] * 2
    v_b = [K.sb("v_b%d" % i, [128, 512], BF16) for i in range(1)] * 2
    qdT2 = [K.sb("qdT%d" % i, [128, 2, 4, WMAX], BF16) for i in range(2)]
    dacc = [K.sb("dacc%d" % i, [128, 2, WMAX], F32) for i in range(2)]
    ones_ff = K.sb("ones_ff", [128, 128], F32)
    kstage = [K.sb("kstage%d" % i, [128, 4, 128], BF16) for i in range(1)] * 2
    NKV = 2
    kts = [K.sb("kts%d" % i, [128, 512], BF16) for i in range(NKV)]
    vts = [K.sb("vts%d" % i, [128, 4, 128], BF16) for i in range(NKV)]
    NPT = 3
    pTt = [K.sb("pTt%d" % i, [128, 2, WMAX], BF16) for i in range(NPT)]
    rl = [K.sb("rl%d" % i, [128, WMAX], F32) for i in range(2)]
    av = rl
    att = K.sb("att", [128, WMAX], F32)
    mixT2 = [K.sb("mixT%d" % i, [128, 8, WMAX], BF16) for i in range(2)]
    qmT = K.sb("qmT", [128, 4, WMAX], BF16)
    omT = K.sb("omT", [128, 4, WMAX], BF16)
    mkT = K.sb("mkT", [128, 4, 256], BF16)
    mv_b = K.sb("mv_b", [128, 2, 512], BF16)
    mk_b = K.sb("mk_b", [128, 512], BF16)
    uT = K.sb("uT", [128, 16, WMAX], BF16)
    rtmp = [att, rl[1]]
    print("SBUF bytes/partition:", K.sb_bytes)

    cnt = {"e": 0}

    def rot(lst, key):
        i = cnt.get(key, 0)
        cnt[key] = i + 1
        return lst[i % len(lst)]

    def ev3(i):
        return ("act", "dve")[i % 2]

    K.dma(ident_f[:], d_ident[:], ident_f)
    K.dma(maskneg[:], d_maskneg[:], maskneg)
    K.dma(noti[:], d_noti[:], noti)
    K.dma(sel[:], d_sel[:], sel)
    K.dma(wab_f[:], d_wab[:], wab_f)
    K.dma(gcols[:].rearrange("p a b -> p (a b)"), d_gcols[:], gcols)
    K.dma(gfin[:], d_gfin[:], gfin)
    K.dma(convw[:].rearrange("p a b -> p (a b)"), d_convw[:], convw)
    K.dma(smallp[:], d_small[:], smallp)
    K.dma(lamt[:], d_lam[:], lamt)
    K.copy("dve", ident_b[:], ident_f[:], [ident_f], [ident_b])
    K.memset("pool", ones_b[:], 1.0, [ones_b])
    K.memset("pool", onesm_b[:], 1.0 / 128.0, [onesm_b])
    K.memset("pool", ones_f[:], 1.0, [ones_f])
    K.memset("pool", ones_ff[:], 1.0, [ones_ff])
    K.memset("pool", qdT2[0][:].rearrange("p a b c -> p (a b c)"), 0.0, [qdT2[0]])
    K.memset("pool", qdT2[1][:].rearrange("p a b c -> p (a b c)"), 0.0, [qdT2[1]])
    K.copy("dve", wab[:].rearrange("p a b -> p (a b)"), wab_f[:], [wab_f], [wab])
    for gi in range(4):
        for kc in range(8):
            K.ts(("dve", "pool")[kc % 2], G[gi][:, kc, :], ones_b[:, :], gcols[:, gi, kc:kc + 1], ALU.mult,
                 [ones_b, gcols], [G[gi]])
    for blk in range(12):
        for tap in range(4):
            K.ts(("dve", "pool")[tap % 2], diagw[:, blk * 4 + tap, :], ident_f[:, :], convw[:, blk, tap:tap + 1],
                 ALU.mult, [ident_f, convw], [diagw])
    K.act(derived[:, 0:1], smallp[:, 1:2], AF.Copy, [smallp], [derived], scale=0.8)
    K.act(derived[0:4, 2:3], smallp[0:4, 2:3], AF.Exp, [smallp], [derived])
    K.ts("dve", derived[0:4, 2:3], derived[0:4, 2:3], -1.0, ALU.mult, [derived], [derived])
    K.memset("dve", lamw[:], 0.0, [lamw])
    lamprod = att
    lv = lamt[:, :].rearrange("p (a b) -> p a b", b=64)
    lpv = att[0:1, 0:128].rearrange("p (a b) -> p a b", b=64)
    K.tt("dve", lpv[:, 0:1, :], lv[:, 0:1, :], lv[:, 1:2, :], ALU.mult, [lamt], [lamprod])
    K.tt("dve", lpv[:, 1:2, :], lv[:, 2:3, :], lv[:, 3:4, :], ALU.mult, [lamt], [lamprod])
    K.op("dve", lambda: nc.vector.tensor_reduce(lamw[:, 0:2], lpv, AX.X, ALU.add), [lamprod], [lamw])
    K.act(lamw[:, 2:4], lamw[:, 0:2], AF.Exp, [lamw], [lamw])
    K.tt("dve", lamw[:, 4:5], lamw[:, 3:4], lamw[:, 2:3], ALU.subtract, [lamw], [lamw])
    K.ts("dve", lamw[:, 5:6], lamw[:, 4:5], -0.2, ALU.add, [lamw], [lamw])
    ps = K.psget()
    K.mm(ps[:, 0:2], ones_f[0:1, 0:128], lamw[0:1, 5:7], [ones_f, lamw], [ps])
    K.copy("dve", derived[:, 1:2], ps[:, 0:1], [ps], [derived])
    K.psput(ps)

    if stage == 1:
        K.finish()
        return K
    uT_ob = Tile(uT.t[:].rearrange("p a b -> p (a b)"), uT.tr)
    for c in range(NCHUNK):
        ob = uT_ob
        for half in range(2):
            stg = wring[(c * 2 + half) % 2]
            stg_f = stg.t[:].bitcast(F32)
            K.dma(stg_f, d_wch[c, :, half * 2048:(half + 1) * 2048], stg)
            en = ev3(c * 2 + half)
            K.copy(en, ob[:, half * 2048:(half + 1) * 2048], stg_f, [stg], [ob])
        K.dma(s_wbf[c], ob[:], s_wbf, ob)

    if stage == 2:
        K.finish()
        return K
    ws = {"pos": 0, "issued": 0, "total": 0}

    def ws_topup():
        while ws["issued"] < min(ws["pos"] + depth_pref, ws["total"]):
            i = ws["issued"]
            slot = wring[i % NSLOT]
            K.dma(slot[:], s_wbf[ws["seq"][i]], slot, s_wbf)
            ws["issued"] += 1

    def ws_next(expect):
        assert ws["seq"][ws["pos"]] == expect, (ws["pos"], expect)
        ws_topup()
        slot = wring[ws["pos"] % NSLOT]
        ws["pos"] += 1
        return slot

    def rms_to_hT(xt, nt, t, gi):
        xb = xn_b[t]
        nm = nrm[t]
        K.act(xb[0:nt, :], xt[0:nt, :], AF.Square, [xt], [xb, nm], accum=nm[0:nt, 0:1])
        K.act(nm[0:nt, 1:2], nm[0:nt, 0:1], AF.Ln, [nm], [nm], scale=1.0 / DM, bias=EPS)
        K.act(nm[0:nt, 2:3], nm[0:nt, 1:2], AF.Exp, [nm], [nm], scale=-0.5)
        K.act(xb[0:nt, :], xt[0:nt, :], AF.Copy, [xt, nm], [xb], scale=nm[0:nt, 2:3])
        ps = K.psget()
        psb = ps.t[:].bitcast(BF16)
        for kc in range(8):
            K.tr_pe(psb[:, kc * 128:kc * 128 + nt], xb[0:nt, kc * 128:(kc + 1) * 128], ident_b[0:nt, 0:nt],
                    [xb, ident_b], [ps], inc=(kc == 7))
        psv = psb.rearrange("p (a b) -> p a b", b=128)
        K.tt("dve", hT[:, :, t * nt:(t + 1) * nt], psv[:, :, 0:nt], G[gi][:, :, 0:nt], ALU.mult, [ps, G[gi]], [hT])
        K.psput(ps)

    def silu_from_psum(out_ap, out_tile, ps, ps_ap, W):
        sg = rn[0]
        K.act(sg[:, 0:W], ps_ap, AF.Exp, [ps], [sg], scale=-1.0)
        K.act(sg[:, 0:W], sg[:, 0:W], AF.Ln, [sg], [sg], bias=1.0)
        K.act(sg[:, 0:W], sg[:, 0:W], AF.Exp, [sg], [sg], scale=-1.0)
        K.tt("dve", out_ap, ps_ap, sg[:, 0:W], ALU.mult, [ps, sg], [out_tile])

    def rsqrt_mean(ps_ap, W, r_tiles, key):
        a = rot(rn, key + "rn")
        b = rot(rinv, key + "ri")
        K.act(a[:, 0:W], ps_ap, AF.Ln, r_tiles, [a], bias=EPS)
        K.act(b[:, 0:W], a[:, 0:W], AF.Exp, [a], [b], scale=-0.5)
        return b

    def setup_mem_from_tokens(kst_tile_list):
        for kt, (kap, vap, trs) in enumerate(kst_tile_list):
            K.copy("pool", mv_b[:, kt, :], vap, trs, [mv_b])
            K.copy("dve", mk_b[:, :], kap, trs, [mk_b])
            ps = K.psget()
            psb = ps.t[:].bitcast(BF16)
            for h in range(4):
                K.tr_pe(psb[:, h * 128:(h + 1) * 128], mk_b[:, h * 128:(h + 1) * 128], ident_b[:, :],
                        [mk_b, ident_b], [ps], inc=(h == 3))
            K.copy("act", mkT[:, :, kt * 128:(kt + 1) * 128], psb[:, 0:512].rearrange("p (a b) -> p a b", b=128),
                   [ps], [mkT])
            K.psput(ps)

    def prompt_mem():
        for t in range(2):
            K.dma(xin[t][:], d_mem[t * 128:(t + 1) * 128, :], xin[t])
            rms_to_hT(xin[t], 128, t, 3)
        lst = []
        uTf = uT[:].rearrange("p a b -> p (a b)")
        for c in range(2):
            K.dma(uTf[:, :], s_wbf[NSTREAM + c], uT, s_wbf)
            wv = uTf[:, :].rearrange("p (a b) -> p a b", b=512)
            for t in range(2):
                ps = K.psget()
                for kc in range(8):
                    K.mm(ps[:, :], hT[:, kc, t * 128:(t + 1) * 128], wv[:, kc, :], [hT, uT], [ps],
                         start=(kc == 0), stop=(kc == 7))
                K.copy("act" if c == 0 else "dve", memst[t][:, c * 512:(c + 1) * 512], ps[:, :], [ps], [memst[t]])
                K.psput(ps)
        for t in range(2):
            K.dma(o_pmk[t * 128:(t + 1) * 128, :], memst[t][:, 0:512], o_pmk, memst[t])
            K.dma(o_pmv[t * 128:(t + 1) * 128, :], memst[t][:, 512:1024], o_pmv, memst[t])
            lst.append((memst[t][:, 0:512], memst[t][:, 512:1024], [memst[t]]))
        setup_mem_from_tokens(lst)

    def sample_mem(si):
        lst = []
        for t in range(2):
            K.dma(memst[t][:, 0:512], d_cmk[si, t * 128:(t + 1) * 128, :], memst[t])
            K.dma(memst[t][:, 512:1024], d_cmv[si, t * 128:(t + 1) * 128, :], memst[t])
            lst.append((memst[t][:, 0:512], memst[t][:, 512:1024], [memst[t]]))
        setup_mem_from_tokens(lst)

    def kv_to_scratch(kb_tile, kb_ap, vb_tile, vb_ap, nt, s_kT, s_v, kpos, ti):
        K.dma(s_v[kpos:kpos + nt, :], vb_ap, s_v, vb_tile)
        ps = K.psget()
        psb = ps.t[:].bitcast(BF16)
        for h in range(4):
            K.tr_pe(psb[:, h * 128:h * 128 + nt], kb_ap[:, h * 128:(h + 1) * 128], ident_b[0:nt, 0:nt],
                    [kb_tile, ident_b], [ps], inc=(h == 3))
        kst = kstage[ti]
        K.copy("act", kst[:, :, 0:nt], psb[:, 0:512].rearrange("p (a b) -> p a b", b=128)[:, :, 0:nt], [ps], [kst])
        K.psput(ps)
        K.dma(s_kT[:, :, kpos:kpos + nt].rearrange("h p t -> p h t"), kst[:, :, 0:nt], s_kT, kst)

    class Stop(Exception):
        pass

    def chk(n):
        if stage == n:
            raise Stop()

    def drive(g1, g2, n1, n2):
        d1 = d2 = 0
        a1 = a2 = True
        while a1 or a2:
            pick1 = a1 and ((not a2) or (d1 * n2 <= d2 * n1))
            if pick1:
                try:
                    next(g1)
                    d1 += 1
                except StopIteration:
                    a1 = False
            else:
                try:
                    next(g2)
                    d2 += 1
                except StopIteration:
                    a2 = False

    def run_seq(d_x, o_y, o_dk, o_dv, o_sg, o_sc, d_rope, s_kT, s_v, T, nt, NT, past, chunk, nlev,
                init_state_ap, init_conv_ap):
        W = nt * NT
        nST = T // W
        if init_state_ap is None:
            for h in range(4):
                K.memset("pool", S_f[h][:], 0.0, [S_f[h]])
                K.memset("pool", S_b[h][:], 0.0, [S_b[h]])
            K.memset("pool", xpre[:, :, 0:3], 0.0, [xpre])
        else:
            for h in range(4):
                K.dma(S_f[h][:], init_state_ap[h], S_f[h])
                K.copy("pool", S_b[h][:], S_f[h][:], [S_f[h]], [S_b[h]])
            for blk in range(12):
                K.dma(convst[:, blk, :], init_conv_ap[:, blk * 128:(blk + 1) * 128].rearrange("t p -> p t"),
                      convst, allow_slow_non_contiguous=True)
            K.copy("dve", xpre[:, :, 0:3], convst[:, :, :], [convst], [xpre])

        def make_st(st):
            tok0 = st * W
            last_st = (st == nST - 1)
            xc = xin2[st % 2]
            mixc = mixT2[st % 2]
            qdT = qdT2[st % 2]

            def front():
                for t in range(NT):
                    K.dma(xc[t][0:nt, :], d_x[tok0 + t * nt: tok0 + (t + 1) * nt, :], xc[t])
                    rms_to_hT(xc[t], nt, t, 0)
                chk(10)
                for c in range(4):
                    wt = ws_next(c)
                    wv = wt[:].rearrange("p (a b) -> p a b", b=512)
                    for j in range(4):
                        blk = c * 4 + j
                        ps = K.psget()
                        for kc in range(8):
                            K.mm(ps[:, 0:W], wv[:, kc, j * 128:(j + 1) * 128], hT[:, kc, 0:W], [wt, hT], [ps],
                                 start=(kc == 0), stop=(kc == 7))
                        if blk < 12:
                            K.copy("act" if blk % 2 == 0 else "dve", xpre[:, blk, 3:3 + W], ps[:, 0:W], [ps], [xpre])
                            if last_st:
                                K.copy("dve", convst[:, blk, :], ps[:, W - 3:W], [ps], [convst])
                        else:
                            silu_from_psum(zs[:, blk - 12, 0:W], zs, ps, ps[:, 0:W], W)
                        K.psput(ps)
                        yield
                if last_st:
                    for blk in range(12):
                        K.dma(o_sc[0][:, blk * 128:(blk + 1) * 128].rearrange("t p -> p t"), convst[:, blk, :], o_sc[1], convst,
                              allow_slow_non_contiguous=True)
                chk(11)
                psA = K.psget()
                psB = K.psget()
                for kc in range(8):
                    K.mm(psA[0:4, 0:W], wab[:, kc, 0:4], hT[:, kc, 0:W], [wab, hT], [psA], start=(kc == 0), stop=(kc == 7))
                for kc in range(8):
                    K.mm(psB[0:4, 0:W], wab[:, kc, 4:8], hT[:, kc, 0:W], [wab, hT], [psB], start=(kc == 0), stop=(kc == 7))
                K.act(abT[:, 0, 0:W], psB[0:4, 0:W], AF.Exp, [psB], [abT], scale=-1.0)
                K.act(abT[:, 0, 0:W], abT[:, 0, 0:W], AF.Ln, [abT], [abT], bias=1.0)
                K.act(abT[:, 0, 0:W], abT[:, 0, 0:W], AF.Exp, [abT], [abT], scale=-1.0)
                K.act(abT[:, 1, 0:W], psA[0:4, 0:W], AF.Exp, [psA, smallp], [abT], bias=smallp[0:4, 3:4])
                K.act(abT[:, 1, 0:W], abT[:, 1, 0:W], AF.Ln, [abT], [abT], bias=1.0)
                K.ts("dve", abT[:, 1, 0:W], abT[:, 1, 0:W], derived[0:4, 2:3], ALU.mult, [abT, derived], [abT])
                K.psput(psA)
                yield
                K.psput(psB)
                yield
                for t in range(NT):
                    K.op("dve", lambda t=t: nc.vector.tensor_tensor_scan(abT[:, 2, t * nt:(t + 1) * nt], ones_f[:, 0:nt],
                                                                         abT[:, 1, t * nt:(t + 1) * nt], 0.0, ALU.mult, ALU.add),
                         [ones_f, abT], [abT])
                for h in range(4):
                    ps = K.psget()
                    K.mm(ps[:, 0:W], sel[0:4, h * 128:(h + 1) * 128], abT[:, 2, 0:W], [sel, abT], [ps])
                    K.mm(ps[:, 256:256 + W], sel[0:4, h * 128:(h + 1) * 128], abT[:, 0, 0:W], [sel, abT], [ps])
                    K.copy("dve", GCrow[:, h, 0:W], ps[:, 0:W], [ps], [GCrow])
                    K.act(EGrow[:, h, 0:W], ps[:, 0:W], AF.Exp, [ps], [EGrow])
                    K.copy("dve", BROW[:, h, 0:W], ps[:, 256:256 + W], [ps], [BROW])
                    K.psput(ps)
                    yield
                for t in range(NT):
                    ps = K.psget()
                    K.mm(ps[0:nt, 0:4], abT[:, 2, t * nt:(t + 1) * nt], ident_f[0:4, 0:4], [abT, ident_f], [ps])
                    K.mm(ps[0:nt, 4:8], abT[:, 0, t * nt:(t + 1) * nt], ident_f[0:4, 0:4], [abT, ident_f], [ps])
                    cl = cols[t]
                    K.copy("dve", cl[0:nt, 0:8], ps[0:nt, 0:8], [ps], [cl])
                    K.psput(ps)
                    yield
                    K.act(cl[0:nt, 8:12], cl[0:nt, 0:4], AF.Exp, [cl], [cl])
                    K.tt("dve", cl[0:nt, 12:16], cl[0:nt, 4:8], cl[0:nt, 8:12], ALU.mult, [cl], [cl])
                    lastc = t * nt + nt - 1
                    for h in range(4):
                        K.act(cl[0:nt, 16 + h:17 + h], cl[0:nt, h:h + 1], AF.Exp, [cl, GCrow], [cl], scale=-1.0,
                              bias=GCrow[0:nt, h, lastc:lastc + 1])
                chk(12)
                for ci, c in enumerate((4, 5, 6)):
                    wt = ws_next(c)
                    wv = wt[:].rearrange("p (a b) -> p a b", b=512)
                    for t in range(NT):
                        ps = K.psget()
                        for kc in range(8):
                            K.mm(ps[0:nt, :], hT[:, kc, t * nt:(t + 1) * nt], wv[:, kc, :], [hT, wt], [ps],
                                 start=(kc == 0), stop=(kc == 7))
                        if ci < 2:
                            K.copy("act", qk_st[t][0:nt, ci * 512:(ci + 1) * 512], ps[0:nt, :], [ps], [qk_st[t]])
                        else:
                            K.copy("dve", v_st[t][0:nt, :], ps[0:nt, :], [ps], [v_st[t]])
                        K.psput(ps)
                        yield
                for t in range(NT):
                    qs = qk_st[t]
                    K.dma(cs[t][0:nt, :], d_rope[tok0 + t * nt: tok0 + (t + 1) * nt, :], cs[t])
                    xv = qs[0:nt, :].rearrange("p (g d) -> p g d", d=64)
                    x1 = xv[:, :, 0:8]
                    x2 = xv[:, :, 8:16]
                    cosv = cs[t][0:nt, 0:128].rearrange("p (g d) -> p g d", d=8)
                    sinv = cs[t][0:nt, 128:256].rearrange("p (g d) -> p g d", d=8)
                    rtt = [ytmp[0], ytmp[0], ytmp[1], ytmp[1]]
                    rt = [rtt[i][0:nt, (i % 2) * 128:(i % 2) * 128 + 128].rearrange("p (g d) -> p g d", d=8) for i in range(4)]
                    K.tt("dve", rt[0], x1, cosv, ALU.mult, [qs, cs[t]], [ytmp[0], ytmp[1]])
                    K.tt("dve", rt[1], x2, sinv, ALU.mult, [qs, cs[t]], [ytmp[0], ytmp[1]])
                    K.tt("dve", rt[2], x2, cosv, ALU.mult, [qs, cs[t]], [ytmp[0], ytmp[1]])
                    K.tt("dve", rt[3], x1, sinv, ALU.mult, [qs, cs[t]], [ytmp[0], ytmp[1]])
                    K.tt("dve", x1, rt[0], rt[1], ALU.subtract, [ytmp[0], ytmp[1]], [qs])
                    K.tt("dve", x2, rt[2], rt[3], ALU.add, [ytmp[0], ytmp[1]], [qs])
                    K.dma(o_dk[0][tok0 + t * nt: tok0 + (t + 1) * nt, :], qs[0:nt, 512:1024], o_dk[1], qs)
                    K.dma(o_dv[0][tok0 + t * nt: tok0 + (t + 1) * nt, :], v_st[t][0:nt, :], o_dv[1], v_st[t])
                    K.copy("dve", qk_b[t][0:nt, :], qs[0:nt, :], [qs], [qk_b[t]])
                    K.copy("act", v_b[t][0:nt, :], v_st[t][0:nt, :], [v_st[t]], [v_b[t]])
                    kv_to_scratch(qk_b[t], qk_b[t][0:nt, 512:1024], v_b[t], v_b[t][0:nt, :], nt, s_kT, s_v,
                                  past + tok0 + t * nt, t)
                    ps = K.psget()
                    psb = ps.t[:].bitcast(BF16)
                    for h in range(4):
                        K.tr_pe(psb[:, h * 128:h * 128 + nt], qk_b[t][0:nt, h * 128:(h + 1) * 128], ident_b[0:nt, 0:nt],
                                [qk_b[t], ident_b], [ps], inc=(h == 3))
                    for s_ in range(2):
                        K.act(qdT[64 * s_:64 * s_ + 64, s_, :, t * nt:(t + 1) * nt],
                              psb[64 * s_:64 * s_ + 64, 0:512].rearrange("p (a b) -> p a b", b=128)[:, :, 0:nt],
                              AF.Copy, [ps], [qdT], scale=0.125)
                    K.psput(ps)
                    yield
                yield

            def gdn_gen():
                chk(13)
                for blk in range(12):
                    ps = K.psget()
                    for tap in range(4):
                        K.mm(ps[:, 0:W], diagw[:, blk * 4 + tap, :], xpre[:, blk, tap:tap + W], [diagw, xpre], [ps],
                             start=(tap == 0), stop=(tap == 3))
                    h = blk % 4
                    if blk >= 8:
                        silu_from_psum(vT[:, h, 0:W], vT, ps, ps[:, 0:W], W)
                        K.psput(ps)
                        yield
                        continue
                    yt = rot(ytmp, "yt")
                    sq = rot(sqtmp, "sq")
                    silu_from_psum(yt[:, 0:W], yt, ps, ps[:, 0:W], W)
                    K.psput(ps)
                    yield
                    K.tt("pool", sq[:, 0:W], yt[:, 0:W], yt[:, 0:W], ALU.mult, [yt], [sq])
                    ps2 = K.psget()
                    K.mm(ps2[:, 0:W], ones_b[:, :], sq[:, 0:W], [ones_b, sq], [ps2])
                    ri = rsqrt_mean(ps2[:, 0:W], W, [ps2], "l2")
                    K.psput(ps2)
                    yield
                    if blk < 4:
                        K.stt(qnT[:, h, 0:W], yt[:, 0:W], float(128 ** -0.5), ri[:, 0:W], ALU.mult, ALU.mult,
                              [yt, ri], [qnT])
                        K.tt("pool", qgT[:, h, 0:W], qnT[:, h, 0:W], EGrow[:, h, 0:W], ALU.mult, [qnT, EGrow], [qgT])
                    else:
                        K.tt("dve", knT[:, h, 0:W], yt[:, 0:W], ri[:, 0:W], ALU.mult, [yt, ri], [knT])
                K.copy("pool", xpre[:, :, 0:3], xpre[:, :, W:W + 3], [xpre], [xpre])
                chk(14)
                ITEMS = [(t, h) for t in range(NT) for h in range(4)]
                for (t, h) in ITEMS:
                    th = (t, h)
                    cl = cols[t]
                    sl = slice(t * nt, (t + 1) * nt)
                    ps = K.psget()
                    psb = ps.t[:].bitcast(BF16)
                    K.tr_pe(psb[0:nt, 0:128], knT[:, h, sl], ident_b[:, :], [knT, ident_b], [ps], inc=False)
                    K.tr_pe(psb[0:nt, 128:256], vT[:, h, sl], ident_b[:, :], [vT, ident_b], [ps], inc=True)
                    K.act(kd[th][0:nt, :], psb[0:nt, 0:128], AF.Copy, [ps, cl], [kd[th]], scale=cl[0:nt, 16 + h:17 + h])
                    K.act(kbg[th][0:nt, :], psb[0:nt, 0:128], AF.Copy, [ps, cl], [kbg[th]], scale=cl[0:nt, 12 + h:13 + h])
                    K.act(vb[th][0:nt, :], psb[0:nt, 128:256], AF.Copy, [ps, cl], [vb[th]], scale=cl[0:nt, 4 + h:5 + h])
                    K.psput(ps)
                    yield
                for (t, h) in ITEMS:
                    th = (t, h)
                    cl = cols[t]
                    sl = slice(t * nt, (t + 1) * nt)
                    dtm, dec, nb2 = Qm[th][1], Rm[th][1], TTm[th]
                    K.stt(dtm[0:nt, 0:nt], GCrow[0:nt, h, sl], cl[0:nt, h:h + 1], maskneg[0:nt, 0:nt], ALU.subtract, ALU.add,
                          [GCrow, cl, maskneg], [dtm])
                    K.act(dec[0:nt, 0:nt], dtm[0:nt, 0:nt], AF.Exp, [dtm], [dec])
                    K.tt("pool", dtm[0:nt, 0:nt], dec[0:nt, 0:nt], BROW[0:nt, h, sl], ALU.mult, [dec, BROW], [dtm])
                    K.tt("pool", nb2[0:nt, 0:nt], dtm[0:nt, 0:nt], noti[0:nt, 0:nt], ALU.mult, [dtm, noti], [nb2])
                yield
                for (t, h) in ITEMS:
                    th = (t, h)
                    sl = slice(t * nt, (t + 1) * nt)
                    dec, nb2 = Rm[th][1], TTm[th]
                    ps = K.psget()
                    K.mm(ps[0:nt, 0:nt], knT[:, h, sl], knT[:, h, sl], [knT], [ps])
                    K.mm(ps[0:nt, 128:128 + nt], knT[:, h, sl], qnT[:, h, sl], [knT, qnT], [ps])
                    K.tt("dve", aqkT[th][0:nt, 0:nt], ps[0:nt, 128:128 + nt], dec[0:nt, 0:nt], ALU.mult, [ps, dec], [aqkT[th]])
                    K.stt(Qm[th][0][0:nt, 0:nt], ps[0:nt, 0:nt], -1.0, nb2[0:nt, 0:nt], ALU.mult, ALU.mult, [ps, nb2], [Qm[th][0]])
                    K.psput(ps)
                    yield
                for (t, h) in ITEMS:
                    th = (t, h)
                    ps = K.psget()
                    K.tr_pe(ps[0:nt, 0:nt], Qm[th][0][0:nt, 0:nt], ident_f[0:nt, 0:nt], [Qm[th][0], ident_f], [ps])
                    K.copy("act", Rm[th][0][0:nt, 0:nt], ps[0:nt, 0:nt], [ps], [Rm[th][0]])
                    K.psput(ps)
                    yield
                    K.tt("pool", TTm[th][0:nt, 0:nt], Qm[th][0][0:nt, 0:nt], ident_f[0:nt, 0:nt], ALU.add,
                         [Qm[th][0], ident_f], [TTm[th]])
                chk(15)
                for lev in range(1, nlev + 1):
                    a = (lev - 1) % 2
                    b = lev % 2
                    lastlev = (lev == nlev)
                    for t in range(NT):
                        for h in range(4):
                            th = (t, h)
                            Q0, R0, Q1, R1 = Qm[th][a], Rm[th][a], Qm[th][b], Rm[th][b]
                            ps = K.psget()
                            if not lastlev:
                                K.mm(ps[0:nt, 0:nt], R0[0:nt, 0:nt], Q0[0:nt, 0:nt], [R0, Q0], [ps])
                            K.mm(ps[0:nt, 128:128 + nt], Q0[0:nt, 0:nt], R0[0:nt, 0:nt], [R0, Q0], [ps])
                            if not lastlev:
                                K.copy("act", Q1[0:nt, 0:nt], ps[0:nt, 0:nt], [ps], [Q1])
                            K.copy("dve", R1[0:nt, 0:nt], ps[0:nt, 128:128 + nt], [ps], [R1])
                            K.psput(ps)
                            yield
                    for t in range(NT):
                        for h in range(4):
                            th = (t, h)
                            R1, TTt = Rm[th][b], TTm[th]
                            ps = K.psget()
                            K.mm(ps[0:nt, 0:nt], R1[0:nt, 0:nt], TTt[0:nt, 0:nt], [R1, TTt], [ps])
                            K.tt("dve", TTt[0:nt, 0:nt], ps[0:nt, 0:nt], TTt[0:nt, 0:nt], ALU.add, [ps, TTt], [TTt])
                            K.psput(ps)
                            yield
                chk(16)
                for t in range(NT):
                    sl = slice(t * nt, (t + 1) * nt)
                    lastc = t * nt + nt - 1
                    for h in range(4):
                        th = (t, h)
                        TTt = TTm[th]
                        ps = K.psget()
                        K.mm(ps[:, 0:nt], kbg[th][0:nt, :], TTt[0:nt, 0:nt], [kbg[th], TTt], [ps])
                        nw = rot(nwT, "nw")
                        K.act(nw[:, 0:nt], ps[:, 0:nt], AF.Copy, [ps], [nw], scale=-1.0)
                        K.psput(ps)
                        yield
                        ps = K.psget()
                        K.mm(ps[0:nt, 0:128], TTt[0:nt, 0:nt], vb[th][0:nt, :], [TTt, vb[th]], [ps], start=True, stop=False)
                        K.mm(ps[0:nt, 0:128], nw[:, 0:nt], S_f[h][:, :], [nw, S_f[h]], [ps], start=False, stop=True)
                        vn = rot(vnew, "vn")
                        K.copy("act", vn[0:nt, :], ps[0:nt, 0:128], [ps], [vn])
                        K.psput(ps)
                        yield
                        ps = K.psget()
                        K.mm(ps[:, 0:nt], S_b[h][:, :], qgT[:, h, sl], [S_b[h], qgT], [ps], start=True, stop=False)
                        K.mm(ps[:, 0:nt], vn[0:nt, :], aqkT[th][0:nt, 0:nt], [vn, aqkT[th]], [ps], start=False, stop=True)
                        K.copy("dve", oT[:, h, sl], ps[:, 0:nt], [ps], [oT])
                        K.psput(ps)
                        yield
                        ps = K.psget()
                        K.mm(ps[:, 0:128], kd[th][0:nt, :], vn[0:nt, :], [kd[th], vn], [ps])
                        K.ts("pool", S_f[h][:, :], S_f[h][:, :], EGrow[:, h, lastc:lastc + 1], ALU.mult, [S_f[h], EGrow], [S_f[h]])
                        K.tt("dve", S_f[h][:, :], ps[:, 0:128], S_f[h][:, :], ALU.add, [ps, S_f[h]], [S_f[h]])
                        K.psput(ps)
                        yield
                        K.copy("pool", S_b[h][:, :], S_f[h][:, :], [S_f[h]], [S_b[h]])
                if last_st:
                    for h in range(4):
                        K.dma(o_sg[0][h], S_f[h][:, :], o_sg[1], S_f[h])
                chk(17)
                for h in range(4):
                    sq = rot(sqtmp, "sq")
                    K.act(sq[:, 0:W], oT[:, h, 0:W], AF.Square, [oT], [sq])
                    ps = K.psget()
                    K.mm(ps[:, 0:W], onesm_b[:, :], sq[:, 0:W], [onesm_b, sq], [ps])
                    ri = rsqrt_mean(ps[:, 0:W], W, [ps], "gn")
                    K.psput(ps)
                    yield
                    yt = rot(ytmp, "yt")
                    K.stt(yt[:, 0:W], oT[:, h, 0:W], smallp[:, 0:1], ri[:, 0:W], ALU.mult, ALU.mult, [oT, smallp, ri], [yt])
                    K.tt("pool", mixc[:, h, 0:W], yt[:, 0:W], zs[:, h, 0:W], ALU.mult, [yt, zs], [mixc])

                yield
            def att_gen():
                chk(18)
                nkeys = past + tok0 + W
                deferred = [None]
                for h in range(4):
                    acc = [K.psget() for _ in range(2)]
                    for s_ in range(2):
                        K.memset("pool", dacc[s_][:, :, 0:W], 0.0, [dacc[s_]])
                    ucnt = [0]
                    nun = [0]
                    k0 = 0
                    first = True
                    pend = []
                    LA = 2

                    def flush_one():
                        vt_u, a_u, kn_u, q0_u, pt_u, first_u, lastk_u = pend.pop(0)
                        for s_u in range(2):
                            K.mm(acc[s_u][:, q0_u:W], vt_u[0:kn_u, a_u, :], pt_u[0:kn_u, s_u, q0_u:W], [vt_u, pt_u], [acc[s_u]],
                                 start=first_u, stop=lastk_u, inc=True)
                        da = dacc[ucnt[0] % 2]
                        ucnt[0] += 1
                        K.tt("dve", da[0:kn_u, :, q0_u:W], da[0:kn_u, :, q0_u:W], pt_u[0:kn_u, :, q0_u:W], ALU.add,
                             [da, pt_u], [da])
                    while k0 < nkeys:
                        gn = min(512, nkeys - k0)
                        kt_t = rot(kts, "kts")
                        vt_t = rot(vts, "vts")
                        K.dma(kt_t[:, 0:gn], s_kT[h, :, k0:k0 + gn], kt_t, s_kT)
                        nfull = gn // 128
                        if nfull > 0:
                            K.dma(vt_t[:, 0:nfull, :],
                                  s_v[k0:k0 + nfull * 128, h * 128:(h + 1) * 128].rearrange("(a p) d -> p a d", p=128),
                                  vt_t, s_v)
                        rem = gn - nfull * 128
                        if rem > 0:
                            K.dma(vt_t[0:rem, nfull, :], s_v[k0 + nfull * 128:k0 + gn, h * 128:(h + 1) * 128], vt_t, s_v)
                        nti = (gn + 127) // 128
                        for a in range(nti):
                            kn = min(128, gn - a * 128)
                            kpos = k0 + a * 128
                            q0 = 0
                            diag = False
                            if past == 0:
                                rel = kpos - tok0
                                if rel >= 0:
                                    q0 = rel
                                    diag = True
                            lastk = (kpos + kn >= nkeys)
                            pss = K.psget()
                            pssv = pss[:, :].rearrange("p (s q) -> p s q", s=2)
                            for s in range(2):
                                K.mm(pssv[0:kn, s, q0:W], kt_t[:, a * 128:a * 128 + kn],
                                     qdT[:, s, h, q0:W], [kt_t, qdT], [pss])
                            pt = rot(pTt, "pt")
                            K.act(pt[0:kn, :, q0:W], pssv[0:kn, :, q0:W], AF.Exp, [pss], [pt])
                            K.psput(pss)
                            yield
                            if diag and kn > chunk:
                                K.memset("pool", pt[chunk:kn, :, q0:q0 + chunk], 0.0, [pt])
                            pend.append((vt_t, a, kn, q0, pt, first, lastk))
                            if len(pend) > LA:
                                flush_one()
                            first = False
                            nun[0] += 1
                            if nun[0] == 3 and deferred[0] is not None:
                                deferred[0]()
                                deferred[0] = None
                        k0 += gn
                    while pend:
                        flush_one()
                    K.tt("dve", dacc[0][:, :, 0:W], dacc[0][:, :, 0:W], dacc[1][:, :, 0:W], ALU.add, [dacc[0], dacc[1]], [dacc[0]])
                    psl = K.psget()
                    for s in range(2):
                        K.mm(psl[:, s * 256:s * 256 + W], ones_ff[:, :], dacc[0][:, s, 0:W], [ones_ff, dacc[0]], [psl])

                    def epi(h=h, acc=acc, psl=psl):
                        for s in range(2):
                            K.act(rl[s][:, 0:W], psl[:, s * 256:s * 256 + W], AF.Ln, [psl], [rl[s]])
                            K.act(rl[s][:, 0:W], rl[s][:, 0:W], AF.Exp, [rl[s]], [rl[s]], scale=-1.0)
                            K.tt("dve", av[s][:, 0:W], acc[s][:, 0:W], rl[s][:, 0:W], ALU.mult, [acc[s], rl[s]], [av[s]])
                        K.psput(psl)
                        for p_ in acc:
                            K.psput(p_)
                        K.stt(att[:, 0:W], av[1][:, 0:W], derived[:, 1:2], av[0][:, 0:W], ALU.mult, ALU.add,
                              [av[0], av[1], derived], [att])
                        sq = sqtmp[0]
                        K.act(sq[:, 0:W], att[:, 0:W], AF.Square, [att], [sq])
                        ps = K.psget()
                        K.mm(ps[:, 0:W], onesm_b[:, :], sq[:, 0:W], [onesm_b, sq], [ps])
                        K.act(rl[0][:, 0:W], ps[:, 0:W], AF.Ln, [ps], [rl[0]], bias=EPS)
                        K.act(rl[1][:, 0:W], rl[0][:, 0:W], AF.Exp, [rl[0]], [rl[1]], scale=-0.5)
                        K.psput(ps)
                        K.stt(mixc[:, 4 + h, 0:W], att[:, 0:W], derived[:, 0:1], rl[1][:, 0:W], ALU.mult, ALU.mult,
                              [att, derived, rl[1]], [mixc])

                    epi()
                    yield
                if deferred[0] is not None:
                    deferred[0]()
                    deferred[0] = None
                yield
            n_att_est = 16 * ((past + tok0 + W + 127) // 128) + 8
            def back_gen():
                chk(19)
                for c in range(2):
                    wt = ws_next(7 + c)
                    wv = wt[:].rearrange("p (a b) -> p a b", b=512)
                    for t in range(NT):
                        ps = K.psget()
                        for kc in range(8):
                            K.mm(ps[0:nt, :], mixc[:, kc, t * nt:(t + 1) * nt], wv[:, kc, :], [mixc, wt], [ps],
                                 start=(kc == 0), stop=(kc == 7))
                        K.tt("dve", xc[t][0:nt, c * 512:(c + 1) * 512], ps[0:nt, :], xc[t][0:nt, c * 512:(c + 1) * 512],
                             ALU.add, [ps, xc[t]], [xc[t]])
                        K.psput(ps)
                        yield
                chk(20)
                for t in range(NT):
                    rms_to_hT(xc[t], nt, t, 1)
                wt = ws_next(9)
                wv = wt[:].rearrange("p (a b) -> p a b", b=512)
                for h in range(4):
                    ps = K.psget()
                    for kc in range(8):
                        K.mm(ps[:, 0:W], wv[:, kc, h * 128:(h + 1) * 128], hT[:, kc, 0:W], [wt, hT], [ps],
                             start=(kc == 0), stop=(kc == 7))
                    K.act(qmT[:, h, 0:W], ps[:, 0:W], AF.Copy, [ps], [qmT], scale=float(128 ** -0.5))
                    K.psput(ps)
                    yield
                for h in range(4):
                    pso = K.psget()
                    psl = K.psget()
                    for kt in range(2):
                        pss = K.psget()
                        K.mm(pss[:, 0:W], mkT[:, h, kt * 128:(kt + 1) * 128], qmT[:, h, 0:W], [mkT, qmT], [pss])
                        pt = rot(pTt, "pt")
                        K.act(pt[:, 0, 0:W], pss[:, 0:W], AF.Exp, [pss], [pt])
                        K.psput(pss)
                        yield
                        K.mm(pso[:, 0:W], mv_b[:, kt, h * 128:(h + 1) * 128], pt[:, 0, 0:W], [mv_b, pt], [pso],
                             start=(kt == 0), stop=(kt == 1), inc=True)
                        K.mm(psl[:, 0:W], ones_b[:, :], pt[:, 0, 0:W], [ones_b, pt], [psl],
                             start=(kt == 0), stop=(kt == 1), inc=True)
                    K.act(rl[0][:, 0:W], psl[:, 0:W], AF.Ln, [psl], [rl[0]])
                    K.act(rl[0][:, 0:W], rl[0][:, 0:W], AF.Exp, [rl[0]], [rl[0]], scale=-1.0)
                    K.tt("dve", omT[:, h, 0:W], pso[:, 0:W], rl[0][:, 0:W], ALU.mult, [pso, rl[0]], [omT])
                    K.psput(pso)
                    yield
                    K.psput(psl)
                    yield
                wt = ws_next(10)
                wv = wt[:].rearrange("p (a b) -> p a b", b=1024)
                for c in range(2):
                    for t in range(NT):
                        ps = K.psget()
                        for kc in range(4):
                            K.mm(ps[0:nt, :], omT[:, kc, t * nt:(t + 1) * nt], wv[:, kc, c * 512:(c + 1) * 512], [omT, wt], [ps],
                                 start=(kc == 0), stop=(kc == 3))
                        K.tt("dve", xc[t][0:nt, c * 512:(c + 1) * 512], ps[0:nt, :], xc[t][0:nt, c * 512:(c + 1) * 512],
                             ALU.add, [ps, xc[t]], [xc[t]])
                        K.psput(ps)
                        yield
                chk(21)
                for t in range(NT):
                    rms_to_hT(xc[t], nt, t, 2)
                accd = {(t, c2): K.psget() for t in range(NT) for c2 in range(2)}
                for half in range(2):
                    for c in range(4):
                        wt = ws_next(11 + half * 8 + c)
                        wv = wt[:].rearrange("p (a b) -> p a b", b=512)
                        for j in range(4):
                            blk = c * 4 + j
                            ps = K.psget()
                            for kc in range(8):
                                K.mm(ps[:, 0:W], wv[:, kc, j * 128:(j + 1) * 128], hT[:, kc, 0:W], [wt, hT], [ps],
                                     start=(kc == 0), stop=(kc == 7))
                            rt_ = rot(rtmp, "yt")
                            K.act(rt_[:, 0:W], ps[:, 0:W], AF.Relu, [ps], [rt_])
                            K.psput(ps)
                            yield
                            K.tt("pool" if blk % 2 == 0 else "dve", uT[:, blk, 0:W], rt_[:, 0:W], rt_[:, 0:W], ALU.mult, [rt_], [uT])
                    for c in range(4):
                        wt = ws_next(11 + half * 8 + 4 + c)
                        wv = wt[:].rearrange("p (a b) -> p a b", b=1024)
                        for t in range(NT):
                            for c2 in range(2):
                                for kk in range(4):
                                    K.mm(accd[(t, c2)][0:nt, :], uT[:, c * 4 + kk, t * nt:(t + 1) * nt],
                                         wv[:, kk, c2 * 512:(c2 + 1) * 512], [uT, wt], [accd[(t, c2)]],
                                         start=(half == 0 and c == 0 and kk == 0), stop=(half == 1 and c == 3 and kk == 3),
                                         inc=(kk == 3))
                for t in range(NT):
                    for c2 in range(2):
                        ps = accd[(t, c2)]
                        K.tt("dve", xc[t][0:nt, c2 * 512:(c2 + 1) * 512], ps[0:nt, :], xc[t][0:nt, c2 * 512:(c2 + 1) * 512],
                             ALU.add, [ps, xc[t]], [xc[t]])
                        K.psput(ps)
                        yield
                chk(22)
                for t in range(NT):
                    xb = xn_b[t]
                    nm = nrm[t]
                    K.act(xb[0:nt, :], xc[t][0:nt, :], AF.Square, [xc[t]], [xb, nm], accum=nm[0:nt, 0:1])
                    K.act(nm[0:nt, 1:2], nm[0:nt, 0:1], AF.Ln, [nm], [nm], scale=1.0 / DM, bias=EPS)
                    K.act(nm[0:nt, 2:3], nm[0:nt, 1:2], AF.Exp, [nm], [nm], scale=-0.5)
                    K.stt(yout[t][0:nt, :], xc[t][0:nt, :], nm[0:nt, 2:3], gfin[0:nt, :], ALU.mult, ALU.mult,
                          [xc[t], nm, gfin], [yout[t]])
                    K.dma(o_y[0][tok0 + t * nt: tok0 + (t + 1) * nt, :], yout[t][0:nt, :], o_y[1], yout[t])


                yield
            return front, gdn_gen, att_gen, back_gen

        sts = [make_st(st) for st in range(nST)]
        for _ in sts[0][0]():
            pass
        for _ in sts[0][1]():
            pass
        for st in range(nST):
            front, gdn_gen, att_gen, back_gen = sts[st]
            if st + 1 < nST:
                natt = 2 * ((past + (st + 1) * W + 127) // 128) * 4 + 8
                drive(att_gen(), sts[st + 1][0](), natt, 45)
                drive(sts[st + 1][1](), back_gen(), 190, 70)
            else:
                for _ in att_gen():
                    pass
                for _ in back_gen():
                    pass

    n_st_total = NS * 1 + TP // 256
    seq = []
    for _s in range(NS):
        seq += list(range(NSTREAM))
    nstp = TP // 256
    seq += list(range(7))
    for _s in range(nstp - 1):
        seq += list(range(7)) + list(range(7, NSTREAM))
    seq += list(range(7, NSTREAM))
    ws["seq"] = seq
    ws["total"] = len(seq)

    def _program():
        for si in range(NS):
            for kt in range(PAST // 128):
                t = kt % 2
                K.dma(qk_st[t][:, 512:1024], d_cdk[si, kt * 128:(kt + 1) * 128, :], qk_st[t])
                K.dma(v_st[t][:, :], d_cdv[si, kt * 128:(kt + 1) * 128, :], v_st[t])
                K.copy("act", qk_b[t][:, 512:1024], qk_st[t][:, 512:1024], [qk_st[t]], [qk_b[t]])
                K.copy("dve", v_b[t][:, :], v_st[t][:, :], [v_st[t]], [v_b[t]])
                kv_to_scratch(qk_b[t], qk_b[t][:, 512:1024], v_b[t], v_b[t][:, :], 128, s_kTs[si], s_vs[si], kt * 128, t)
            sample_mem(si)
            chk(3)
            run_seq(d_xs[si], (o_ys[si], o_ys), (o_sdk[si], o_sdk), (o_sdv[si], o_sdv), (o_ssg[si], o_ssg), (o_ssc[si], o_ssc),
                    d_ropeS, s_kTs[si], s_vs[si], TS_LEN, TS_LEN, 1, PAST, 64, 4, d_sg[si], d_sgc[si])
        chk(4)
        prompt_mem()
        chk(5)
        run_seq(d_xp, (o_yp[:], o_yp), (o_pdk[:], o_pdk), (o_pdv[:], o_pdv), (o_psg[:], o_psg), (o_psc[:], o_psc),
                d_ropeP, s_kTp, s_vp, TP, 128, 2, 0, 64, 6, None, None)

    try:
        _program()
    except Stop:
        pass
    K.finish()
    print("instructions:", K.n_ins, "sems:", K.nsem, {k: e.count for k, e in K.E.items()})
    return K

_CACHE = {}


def _chunks_from_weights(w_in, w_out, w_mq, w_mo, w_up, w_down, w_mkv):
    def colchunk(Wm, c0):
        blk = Wm[:, c0:c0 + 512]
        return blk.reshape(8, 128, 512).transpose(1, 0, 2).reshape(128, 4096)

    def rowchunk(Wm, r0):
        blk = Wm[r0:r0 + 512, :]
        return blk.reshape(4, 128, 1024).transpose(1, 0, 2).reshape(128, 4096)

    ch = []
    for c in range(4):
        ch.append(colchunk(w_in, c * 512))
    for c in range(3):
        ch.append(colchunk(w_in, 2056 + c * 512))
    for c in range(2):
        ch.append(colchunk(w_out, c * 512))
    ch.append(colchunk(w_mq, 0))
    ch.append(rowchunk(w_mo, 0))
    for half in range(2):
        for c in range(4):
            ch.append(colchunk(w_up, (half * 4 + c) * 512))
        for c in range(4):
            ch.append(rowchunk(w_down, (half * 4 + c) * 512))
    for c in range(2):
        ch.append(colchunk(w_mkv, c * 512))
    return np.ascontiguousarray(np.stack(ch, 0).astype(np.float32))


def _rope_table(pos):
    inv = (np.float32(500000.0) ** (-(np.arange(0, 16, 2, dtype=np.float32)) / np.float32(16))).astype(np.float32)
    ang = pos.astype(np.float32)[:, None] * inv[None, :]
    c = np.cos(ang).astype(np.float32)
    s = np.sin(ang).astype(np.float32)
    return np.ascontiguousarray(np.concatenate([np.tile(c, (1, 16)), np.tile(s, (1, 16))], axis=1).astype(np.float32))


def make_in_maps(inp, TP, cores, NS=2):
    f = lambda a: np.ascontiguousarray(np.asarray(a, dtype=np.float32))
    w_in = f(inp["w_in"])[0]
    wch = _chunks_from_weights(w_in, f(inp["w_out"])[0], f(inp["w_mq"])[0], f(inp["w_mo"])[0], f(inp["w_up"])[0],
                               f(inp["w_down"])[0], f(inp["w_mkv"])[0])
    wab = np.ascontiguousarray(w_in[:, 2048:2056].reshape(8, 128, 8).transpose(1, 0, 2).reshape(128, 64))
    gl = [f(inp[k])[0] for k in ("norm_mix_g", "norm_mem_g", "norm_ffn_g", "mem_norm_g")]
    gcols = np.ascontiguousarray(np.stack([g.reshape(8, 128).T for g in gl], 1).reshape(128, 32))
    gfin = np.ascontiguousarray(np.tile(f(inp["final_norm_g"])[None, :], (128, 1)))
    convw = np.ascontiguousarray(f(inp["gdn_conv_w"])[0].T.reshape(12, 128, 4).transpose(1, 0, 2).reshape(128, 48))
    small = np.zeros((128, 8), np.float32)
    small[:, 0] = f(inp["gdn_norm_g"])[0]
    small[:, 1] = f(inp["diff_norm_g"])[0]
    small[0:4, 2] = f(inp["gdn_a_log"])[0]
    small[0:4, 3] = f(inp["gdn_dt_bias"])[0]
    lamp = np.ascontiguousarray(f(inp["diff_lambda"])[0].reshape(1, 256))
    ident = np.eye(128, dtype=np.float32)
    r = np.arange(128)
    maskneg = np.where(r[None, :] >= r[:, None], 0.0, -1.0e5).astype(np.float32)
    noti = (1.0 - ident).astype(np.float32)
    selm = np.zeros((4, 512), np.float32)
    for h in range(4):
        selm[h, h * 128:(h + 1) * 128] = 1.0
    ropeP = _rope_table(np.arange(TP))
    ropeS = _rope_table(1024 + np.arange(32))
    xp = f(inp["x_prompt"]); xs = f(inp["x_sample"]); mem = f(inp["mem_prompt"])
    cdk = f(inp["cache_diff_k"])[0]; cdv = f(inp["cache_diff_v"])[0]
    cmk = f(inp["cache_mem_k"])[0]; cmv = f(inp["cache_mem_v"])[0]
    sg = f(inp["state_gdn"])[0]; sgc = f(inp["state_gdn_conv"])[0]
    maps = []
    for c in cores:
        sl = slice(c * NS, (c + 1) * NS)
        maps.append({
            "x_prompt": np.ascontiguousarray(xp[c, :TP]),
            "x_sample": np.ascontiguousarray(xs[sl]),
            "mem_prompt": np.ascontiguousarray(mem[c]),
            "cache_diff_k": np.ascontiguousarray(cdk[sl].reshape(NS, 1024, 512)),
            "cache_diff_v": np.ascontiguousarray(cdv[sl].reshape(NS, 1024, 512)),
            "cache_mem_k": np.ascontiguousarray(cmk[sl].reshape(NS, 256, 512)),
            "cache_mem_v": np.ascontiguousarray(cmv[sl].reshape(NS, 256, 512)),
            "state_gdn": np.ascontiguousarray(sg[sl]),
            "state_gdn_conv": np.ascontiguousarray(sgc[sl]),
            "wch": wch, "wab": wab, "gcols": gcols, "gfin": gfin, "convw": convw, "smallp": small, "lamp": lamp,
            "ident": ident, "maskneg": maskneg, "noti": noti, "sel": selm, "ropeP": ropeP, "ropeS": ropeS,
        })
    return maps


def kernel(**inputs):
    TP = 8192
    NCORE = 8
    if "K" not in _CACHE:
        _CACHE["K"] = build(TP)
    K = _CACHE["K"]
    maps = make_in_maps(inputs, TP, list(range(NCORE)))
    res = run_bass_kernel_spmd(K.nc, maps, core_ids=list(range(NCORE)))
    R = res.results
    g = lambda name: [np.asarray(r[name], dtype=np.float32) for r in R]
    y_prompt = np.stack(g("y_prompt"), 0)
    y_sample = np.concatenate(g("y_sample"), 0)
    p_state = np.stack(g("p_state_gdn"), 0)[None]
    p_conv = np.stack(g("p_state_gdn_conv"), 0)[None]
    p_dk = np.stack(g("p_diff_k"), 0).reshape(1, NCORE, TP, 4, 128)
    p_dv = np.stack(g("p_diff_v"), 0).reshape(1, NCORE, TP, 4, 128)
    p_mk = np.stack(g("p_mem_k"), 0).reshape(1, NCORE, 256, 4, 128)
    p_mv = np.stack(g("p_mem_v"), 0).reshape(1, NCORE, 256, 4, 128)
    s_state = np.concatenate(g("s_state_gdn"), 0)[None]
    s_conv = np.concatenate(g("s_state_gdn_conv"), 0)[None]
    s_dk = np.concatenate(g("s_diff_k"), 0).reshape(1, 2 * NCORE, 32, 4, 128)
    s_dv = np.concatenate(g("s_diff_v"), 0).reshape(1, 2 * NCORE, 32, 4, 128)
    return (y_prompt, y_sample, p_state, p_conv, p_dk, p_dv, p_mk, p_mv, s_state, s_conv, s_dk, s_dv)
```

```python
import numpy as np
from contextlib import ExitStack
import concourse.bass as bass
import concourse.mybir as mybir
from concourse.bass_utils import run_bass_kernel_spmd

F32 = mybir.dt.float32
BF16 = mybir.dt.bfloat16
AF = mybir.ActivationFunctionType
ALU = mybir.AluOpType
AX = mybir.AxisListType

SAME_ENGINE_SYNC = True


class Track:
    __slots__ = ("name", "w", "re", "rd", "dsem", "dtotal", "excl", "dram")

    def __init__(self, name):
        self.name = name
        self.w = None
        self.re = {}
        self.rd = []
        self.dsem = None
        self.dtotal = 0
        self.excl = False
        self.dram = False


class Tile:
    def __init__(self, t, tr):
        self.t = t
        self.tr = tr

    def __getitem__(self, key):
        return self.t[key]


class Eng:
    def __init__(self, name, h, sem):
        self.name = name
        self.h = h
        self.sem = sem
        self.count = 0
        self.seen = {}


class KB:
    def __init__(self):
        self.nc = bass.Bass("TRN2", target_bir_lowering=False)
        self.es = ExitStack()
        nc = self.nc
        self.E = {}
        for name, h in (("pe", nc.tensor), ("act", nc.scalar), ("dve", nc.vector),
                        ("pool", nc.gpsimd), ("sp", nc.sync)):
            sem = self.es.enter_context(nc.semaphore("prog_" + name))
            self.E[name] = Eng(name, h, sem)
        self.nsem = 5
        self.sb_bytes = 0
        self.n_ins = 0
        self.ps_free = []
        self.all_out_tracks = []

    def sb(self, name, shape, dt):
        t = self.es.enter_context(self.nc.sbuf_tensor("sb_" + name, list(shape), dt))
        nb = int(np.prod(shape[1:])) * (4 if dt == F32 else 2)
        self.sb_bytes += nb
        return Tile(t, Track(name))

    def ps_init(self):
        self.ps_tiles = []
        for i in range(8):
            t = self.es.enter_context(self.nc.psum_tensor("psb%d" % i, [128, 512], F32))
            tl = Tile(t, Track("psb%d" % i))
            tl.tr.excl = True
            self.ps_tiles.append(tl)
        self.ps_free = list(self.ps_tiles)

    def psget(self):
        assert self.ps_free, "out of psum banks"
        return self.ps_free.pop(0)

    def psput(self, p):
        self.ps_free.append(p)

    def dram(self, name, shape, dt, kind="Internal"):
        t = self.nc.dram_tensor(name, list(shape), dt, kind=kind)
        tl = Tile(t.ap(), Track(name))
        tl.tr.dram = True
        return tl

    def newsem(self, name):
        self.nsem += 1
        return self.es.enter_context(self.nc.semaphore(name))

    def _wait(self, eng, ev):
        if ev is None:
            return
        if ev[0] == "e":
            src, idx = ev[1], ev[2]
            if src is eng and (not SAME_ENGINE_SYNC or eng.name in ("pe", "sp")):
                return
            assert src.count >= idx, "wait on unemitted inc %s %d>%d" % (src.name, idx, src.count)
            if eng.seen.get(src.name, 0) >= idx:
                return
            eng.h.wait_ge(src.sem, idx)
            eng.seen[src.name] = idx
            self.n_ins += 1
        else:
            tr = ev[1]
            val = tr.dtotal
            if eng.seen.get(id(tr), 0) >= val:
                return
            eng.h.wait_ge(tr.dsem, val)
            eng.seen[id(tr)] = val
            self.n_ins += 1

    def _pre(self, eng, r, w):
        for t in r:
            self._wait(eng, t.w)
            if t.excl:
                for en, idx in t.re.items():
                    if en != eng.name:
                        self._wait(eng, ("e", self.E[en], idx))
        for t in w:
            self._wait(eng, t.w)
            for en, idx in t.re.items():
                self._wait(eng, ("e", self.E[en], idx))
            for dtr in t.rd:
                self._wait(eng, ("d", dtr))

    def op(self, en, fn, r=(), w=(), inc=True):
        eng = self.E[en]
        r = [x.tr if isinstance(x, Tile) else x for x in r]
        w = [x.tr if isinstance(x, Tile) else x for x in w]
        self._pre(eng, r, w)
        ins = fn()
        self.n_ins += 1
        if inc:
            ins.then_inc(eng.sem, 1)
            eng.count += 1
            idx = eng.count
        else:
            idx = eng.count + 1
        ev = ("e", eng, idx)
        for t in w:
            t.w = ev
            t.re = {}
            t.rd = []
        for t in r:
            if t.re.get(en, 0) < idx:
                t.re[en] = idx
        return ins

    def dma(self, out_ap, in_ap, w, r=None, q="sp", **kw):
        eng = self.E[q]
        w = w.tr if isinstance(w, Tile) else w
        if r is not None:
            r = r.tr if isinstance(r, Tile) else r
        self._pre(eng, [r] if r is not None else [], [] if w.dram else [w])
        if w.dsem is None:
            w.dsem = self.newsem("d_" + w.name)
        ins = eng.h.dma_start(out=out_ap, in_=in_ap, **kw)
        ins.then_inc(w.dsem, 16)
        self.n_ins += 1
        w.dtotal += 16
        w.w = ("d", w)
        w.re = {}
        w.rd = []
        if r is not None:
            if w not in r.rd:
                r.rd.append(w)
        return ins

    def finish(self):
        eng = self.E["sp"]
        for tr in self.all_out_tracks:
            if tr.dsem is not None:
                self._wait(eng, ("d", tr))

    def mm(self, out, lhsT, rhs, r, w, start=True, stop=True, inc=None):
        if inc is None:
            inc = stop
        nc = self.nc
        return self.op("pe", lambda: nc.tensor.matmul(out, lhsT, rhs, start=start, stop=stop), r, w, inc)

    def tr_pe(self, out, in_, ident, r, w, inc=True):
        nc = self.nc
        return self.op("pe", lambda: nc.tensor.transpose(out, in_, ident), r, w, inc)

    def act(self, out, in_, func, r, w, scale=1.0, bias=None, accum=None):
        nc = self.nc
        kw = {}
        if bias is not None:
            kw["bias"] = bias
        if accum is not None:
            kw["accum_out"] = accum
        return self.op("act", lambda: nc.scalar.activation(out, in_, func, scale=scale, **kw), r, w)

    def _ve(self, en):
        return self.nc.vector if en == "dve" else self.nc.gpsimd

    def tt(self, en, out, in0, in1, op, r, w):
        e = self._ve(en)
        return self.op(en, lambda: e.tensor_tensor(out, in0, in1, op), r, w)

    def ts(self, en, out, in0, s1, op0, r, w, s2=None, op1=None):
        e = self._ve(en)
        if op1 is None:
            return self.op(en, lambda: e.tensor_scalar(out, in0, s1, None, op0), r, w)
        return self.op(en, lambda: e.tensor_scalar(out, in0, s1, s2, op0, op1), r, w)

    def stt(self, out, in0, scalar, in1, op0, op1, r, w):
        nc = self.nc
        return self.op("dve", lambda: nc.vector.scalar_tensor_tensor(out, in0, scalar, in1, op0, op1), r, w)

    def copy(self, en, out, in_, r, w):
        if en == "act":
            nc = self.nc
            return self.op("act", lambda: nc.scalar.copy(out, in_), r, w)
        e = self._ve(en)
        return self.op(en, lambda: e.tensor_copy(out, in_), r, w)

    def memset(self, en, ap, val, w):
        e = self._ve(en)
        return self.op(en, lambda: e.memset(ap, val), (), w)

    def recip(self, out, in_, r, w):
        nc = self.nc
        return self.op("dve", lambda: nc.vector.reciprocal(out, in_), r, w)

P = 128
DM = 1024
NCHUNK = 29
NSTREAM = 27
EPS = 1e-6
NEG = -1.0e5
PAST = 1024
TS_LEN = 32


def build(TP, NS=2, depth_pref=2, stage=99):
    K = KB()
    nc = K.nc
    K.ps_init()
    EI = "ExternalInput"
    EO = "ExternalOutput"
    d_xp = K.dram("x_prompt", [TP, DM], F32, EI)
    d_xs = K.dram("x_sample", [NS, TS_LEN, DM], F32, EI)
    d_mem = K.dram("mem_prompt", [256, DM], F32, EI)
    d_cdk = K.dram("cache_diff_k", [NS, PAST, 512], F32, EI)
    d_cdv = K.dram("cache_diff_v", [NS, PAST, 512], F32, EI)
    d_cmk = K.dram("cache_mem_k", [NS, 256, 512], F32, EI)
    d_cmv = K.dram("cache_mem_v", [NS, 256, 512], F32, EI)
    d_sg = K.dram("state_gdn", [NS, 4, 128, 128], F32, EI)
    d_sgc = K.dram("state_gdn_conv", [NS, 3, 1536], F32, EI)
    d_wch = K.dram("wch", [NCHUNK, 128, 4096], F32, EI)
    d_wab = K.dram("wab", [128, 64], F32, EI)
    d_gcols = K.dram("gcols", [128, 32], F32, EI)
    d_gfin = K.dram("gfin", [128, DM], F32, EI)
    d_convw = K.dram("convw", [128, 48], F32, EI)
    d_small = K.dram("smallp", [128, 8], F32, EI)
    d_lam = K.dram("lamp", [1, 256], F32, EI)
    d_ident = K.dram("ident", [128, 128], F32, EI)
    d_maskneg = K.dram("maskneg", [128, 128], F32, EI)
    d_noti = K.dram("noti", [128, 128], F32, EI)
    d_sel = K.dram("sel", [4, 512], F32, EI)
    d_ropeP = K.dram("ropeP", [TP, 256], F32, EI)
    d_ropeS = K.dram("ropeS", [TS_LEN, 256], F32, EI)

    o_yp = K.dram("y_prompt", [TP, DM], F32, EO)
    o_ys = K.dram("y_sample", [NS, TS_LEN, DM], F32, EO)
    o_psg = K.dram("p_state_gdn", [4, 128, 128], F32, EO)
    o_psc = K.dram("p_state_gdn_conv", [3, 1536], F32, EO)
    o_pdk = K.dram("p_diff_k", [TP, 512], F32, EO)
    o_pdv = K.dram("p_diff_v", [TP, 512], F32, EO)
    o_pmk = K.dram("p_mem_k", [256, 512], F32, EO)
    o_pmv = K.dram("p_mem_v", [256, 512], F32, EO)
    o_ssg = K.dram("s_state_gdn", [NS, 4, 128, 128], F32, EO)
    o_ssc = K.dram("s_state_gdn_conv", [NS, 3, 1536], F32, EO)
    o_sdk = K.dram("s_diff_k", [NS, TS_LEN, 512], F32, EO)
    o_sdv = K.dram("s_diff_v", [NS, TS_LEN, 512], F32, EO)
    for o in (o_yp, o_ys, o_psg, o_psc, o_pdk, o_pdv, o_pmk, o_pmv, o_ssg, o_ssc, o_sdk, o_sdv):
        K.all_out_tracks.append(o.tr)

    s_wbf = K.dram("s_wbf", [NCHUNK, 128, 4096], BF16)
    s_kTp = K.dram("s_kTp", [4, 128, TP], BF16)
    s_vp = K.dram("s_vp", [TP, 512], BF16)
    s_kTs = [K.dram("s_kTs%d" % i, [4, 128, PAST + TS_LEN], BF16) for i in range(NS)]
    s_vs = [K.dram("s_vs%d" % i, [PAST + TS_LEN, 512], BF16) for i in range(NS)]

    WMAX = 256
    ident_f = K.sb("ident_f", [128, 128], F32)
    ident_b = K.sb("ident_b", [128, 128], BF16)
    maskneg = K.sb("maskneg", [128, 128], F32)
    noti = K.sb("noti", [128, 128], F32)
    ones_b = K.sb("ones_b", [128, 128], BF16)
    onesm_b = K.sb("onesm_b", [128, 128], BF16)
    ones_f = K.sb("ones_f", [4, 128], F32)
    sel = K.sb("sel", [4, 512], F32)
    wab_f = K.sb("wab_f", [128, 64], F32)
    wab = K.sb("wab", [128, 8, 8], BF16)
    gcols = K.sb("gcols", [128, 4, 8], F32)
    G = [K.sb("G%d" % i, [128, 8, 128], BF16) for i in range(4)]
    gfin = K.sb("gfin", [128, DM], F32)
    convw = K.sb("convw", [128, 12, 4], F32)
    diagw = K.sb("diagw", [128, 48, 128], BF16)
    smallp = K.sb("smallp", [128, 8], F32)
    derived = K.sb("derived", [128, 8], F32)
    lamt = K.sb("lamt", [1, 256], F32)
    lamw = K.sb("lamw", [1, 16], F32)

    NSLOT = 2
    wring = [K.sb("wring%d" % i, [128, 4096], BF16) for i in range(NSLOT)]

    xin2 = [[K.sb("xin%d_%d" % (j, i), [128, DM], F32) for i in range(2)] for j in range(2)]
    xin = xin2[0]
    xn_b = [K.sb("xn_b%d" % i, [128, DM], BF16) for i in range(1)] * 2
    nrm = [K.sb("nrm%d" % i, [128, 4], F32) for i in range(2)]
    hT = K.sb("hT", [128, 8, WMAX], BF16)

    xpre = K.sb("xpre", [128, 12, 3 + WMAX], BF16)
    convst = K.sb("convst", [128, 12, 3], F32)
    zs = K.sb("zs", [128, 4, WMAX], BF16)
    knT = K.sb("knT", [128, 4, WMAX], BF16)
    qnT = K.sb("qnT", [128, 4, WMAX], BF16)
    qgT = K.sb("qgT", [128, 4, WMAX], BF16)
    vT = K.sb("vT", [128, 4, WMAX], BF16)
    oT = K.sb("oT", [128, 4, WMAX], F32)
    GCrow = K.sb("GCrow", [128, 4, WMAX], F32)
    EGrow = K.sb("EGrow", [128, 4, WMAX], F32)
    BROW = K.sb("BROW", [128, 4, WMAX], BF16)
    abT = K.sb("abT", [4, 3, WMAX], F32)
    cols = [K.sb("cols%d" % i, [128, 24], F32) for i in range(2)]
    ytmp = [K.sb("ytmp%d" % i, [128, WMAX], F32) for i in range(2)]
    sqtmp = [K.sb("sqtmp%d" % i, [128, WMAX], BF16) for i in range(2)]
    rn = [K.sb("rn%d" % i, [128, WMAX], F32) for i in range(1)]
    rinv = [K.sb("rinv%d" % i, [128, WMAX], F32) for i in range(1)]
    S_f = [K.sb("S_f%d" % h, [128, 128], F32) for h in range(4)]
    S_b = [K.sb("S_b%d" % h, [128, 128], BF16) for h in range(4)]
    TH = [(t, h) for t in range(2) for h in range(4)]
    kd = {th: K.sb("kd%d%d" % th, [128, 128], BF16) for th in TH}
    kbg = {th: K.sb("kbg%d%d" % th, [128, 128], F32) for th in TH}
    vb = {th: K.sb("vb%d%d" % th, [128, 128], F32) for th in TH}
    aqkT = {th: K.sb("aqkT%d%d" % th, [128, 128], BF16) for th in TH}
    QP = {th: K.sb("QP%d%d" % th, [128, 2, 128], F32) for th in TH}
    RP = {th: K.sb("RP%d%d" % th, [128, 2, 128], F32) for th in TH}
    Qm = {th: [Tile(QP[th].t[:, i, :], QP[th].tr) for i in range(2)] for th in TH}
    Rm = {th: [Tile(RP[th].t[:, i, :], RP[th].tr) for i in range(2)] for th in TH}
    TTm = {th: K.sb("TT%d%d" % th, [128, 128], F32) for th in TH}
    gtmp = None
    nwT = [K.sb("nwT%d" % i, [128, 128], F32) for i in range(2)]
    vnew = [K.sb("vnew%d" % i, [128, 128], BF16) for i in range(2)]

    qk_st = [K.sb("qk_st%d" % i, [128, 1024], F32) for i in range(2)]
    yout = qk_st
    memst = qk_st
    v_st = [K.sb("v_st%d" % i, [128, 512], F32) for i in range(2)]
    cs = [K.sb("cs%d" % i, [128, 256], F32) for i in range(1)] * 2
    ropet = K.sb("ropet", [128, 4, 128], F32)
    qk_b = [K.sb("qk_b%d" % i, [128, 1024], BF16) for i in range(1)] * 2
    v_b = [K.sb("v_b%d" % i, [128, 512], BF16) for i in range(1)] * 2
    qdT = K.sb("qdT", [128, 2, 4, WMAX], BF16)
    dacc = [K.sb("dacc%d" % i, [128, 2, WMAX], F32) for i in range(2)]
    ones_ff = K.sb("ones_ff", [128, 128], F32)
    kstage = [K.sb("kstage%d" % i, [128, 4, 128], BF16) for i in range(1)] * 2
    NKV = 2
    kts = [K.sb("kts%d" % i, [128, 512], BF16) for i in range(NKV)]
    vts = [K.sb("vts%d" % i, [128, 4, 128], BF16) for i in range(NKV)]
    NPT = 3
    pTt = [K.sb("pTt%d" % i, [128, 2, WMAX], BF16) for i in range(NPT)]
    rl = [K.sb("rl%d" % i, [128, WMAX], F32) for i in range(2)]
    av = rl
    att = K.sb("att", [128, WMAX], F32)
    mixT2 = [K.sb("mixT%d" % i, [128, 8, WMAX], BF16) for i in range(2)]
    qmT = K.sb("qmT", [128, 4, WMAX], BF16)
    omT = K.sb("omT", [128, 4, WMAX], BF16)
    mkT = K.sb("mkT", [128, 4, 256], BF16)
    mv_b = K.sb("mv_b", [128, 2, 512], BF16)
    mk_b = K.sb("mk_b", [128, 512], BF16)
    uT = K.sb("uT", [128, 16, WMAX], BF16)
    rtmp = [att, rl[1]]
    print("SBUF bytes/partition:", K.sb_bytes)

    cnt = {"e": 0}

    def rot(lst, key):
        i = cnt.get(key, 0)
        cnt[key] = i + 1
        return lst[i % len(lst)]

    def ev3(i):
        return ("act", "dve")[i % 2]

    K.dma(ident_f[:], d_ident[:], ident_f)
    K.dma(maskneg[:], d_maskneg[:], maskneg)
    K.dma(noti[:], d_noti[:], noti)
    K.dma(sel[:], d_sel[:], sel)
    K.dma(wab_f[:], d_wab[:], wab_f)
    K.dma(gcols[:].rearrange("p a b -> p (a b)"), d_gcols[:], gcols)
    K.dma(gfin[:], d_gfin[:], gfin)
    K.dma(convw[:].rearrange("p a b -> p (a b)"), d_convw[:], convw)
    K.dma(smallp[:], d_small[:], smallp)
    K.dma(lamt[:], d_lam[:], lamt)
    K.copy("dve", ident_b[:], ident_f[:], [ident_f], [ident_b])
    K.memset("pool", ones_b[:], 1.0, [ones_b])
    K.memset("pool", onesm_b[:], 1.0 / 128.0, [onesm_b])
    K.memset("pool", ones_f[:], 1.0, [ones_f])
    K.memset("pool", ones_ff[:], 1.0, [ones_ff])
    K.memset("pool", qdT[:].rearrange("p a b c -> p (a b c)"), 0.0, [qdT])
    K.copy("dve", wab[:].rearrange("p a b -> p (a b)"), wab_f[:], [wab_f], [wab])
    for gi in range(4):
        for kc in range(8):
            K.ts(("dve", "pool")[kc % 2], G[gi][:, kc, :], ones_b[:, :], gcols[:, gi, kc:kc + 1], ALU.mult,
                 [ones_b, gcols], [G[gi]])
    for blk in range(12):
        for tap in range(4):
            K.ts(("dve", "pool")[tap % 2], diagw[:, blk * 4 + tap, :], ident_f[:, :], convw[:, blk, tap:tap + 1],
                 ALU.mult, [ident_f, convw], [diagw])
    K.act(derived[:, 0:1], smallp[:, 1:2], AF.Copy, [smallp], [derived], scale=0.8)
    K.act(derived[0:4, 2:3], smallp[0:4, 2:3], AF.Exp, [smallp], [derived])
    K.ts("dve", derived[0:4, 2:3], derived[0:4, 2:3], -1.0, ALU.mult, [derived], [derived])
    K.memset("dve", lamw[:], 0.0, [lamw])
    lamprod = att
    lv = lamt[:, :].rearrange("p (a b) -> p a b", b=64)
    lpv = att[0:1, 0:128].rearrange("p (a b) -> p a b", b=64)
    K.tt("dve", lpv[:, 0:1, :], lv[:, 0:1, :], lv[:, 1:2, :], ALU.mult, [lamt], [lamprod])
    K.tt("dve", lpv[:, 1:2, :], lv[:, 2:3, :], lv[:, 3:4, :], ALU.mult, [lamt], [lamprod])
    K.op("dve", lambda: nc.vector.tensor_reduce(lamw[:, 0:2], lpv, AX.X, ALU.add), [lamprod], [lamw])
    K.act(lamw[:, 2:4], lamw[:, 0:2], AF.Exp, [lamw], [lamw])
    K.tt("dve", lamw[:, 4:5], lamw[:, 3:4], lamw[:, 2:3], ALU.subtract, [lamw], [lamw])
    K.ts("dve", lamw[:, 5:6], lamw[:, 4:5], -0.2, ALU.add, [lamw], [lamw])
    ps = K.psget()
    K.mm(ps[:, 0:2], ones_f[0:1, 0:128], lamw[0:1, 5:7], [ones_f, lamw], [ps])
    K.copy("dve", derived[:, 1:2], ps[:, 0:1], [ps], [derived])
    K.psput(ps)

    if stage == 1:
        K.finish()
        return K
    uT_ob = Tile(uT.t[:].rearrange("p a b -> p (a b)"), uT.tr)
    for c in range(NCHUNK):
        ob = uT_ob
        for half in range(2):
            stg = wring[(c * 2 + half) % 2]
            stg_f = stg.t[:].bitcast(F32)
            K.dma(stg_f, d_wch[c, :, half * 2048:(half + 1) * 2048], stg)
            en = ev3(c * 2 + half)
            K.copy(en, ob[:, half * 2048:(half + 1) * 2048], stg_f, [stg], [ob])
        K.dma(s_wbf[c], ob[:], s_wbf, ob)

    if stage == 2:
        K.finish()
        return K
    ws = {"pos": 0, "issued": 0, "total": 0}

    def ws_topup():
        while ws["issued"] < min(ws["pos"] + depth_pref, ws["total"]):
            i = ws["issued"]
            slot = wring[i % NSLOT]
            K.dma(slot[:], s_wbf[ws["seq"][i]], slot, s_wbf)
            ws["issued"] += 1

    def ws_next(expect):
        assert ws["seq"][ws["pos"]] == expect, (ws["pos"], expect)
        ws_topup()
        slot = wring[ws["pos"] % NSLOT]
        ws["pos"] += 1
        return slot

    def rms_to_hT(xt, nt, t, gi):
        xb = xn_b[t]
        nm = nrm[t]
        K.act(xb[0:nt, :], xt[0:nt, :], AF.Square, [xt], [xb, nm], accum=nm[0:nt, 0:1])
        K.act(nm[0:nt, 1:2], nm[0:nt, 0:1], AF.Ln, [nm], [nm], scale=1.0 / DM, bias=EPS)
        K.act(nm[0:nt, 2:3], nm[0:nt, 1:2], AF.Exp, [nm], [nm], scale=-0.5)
        K.act(xb[0:nt, :], xt[0:nt, :], AF.Copy, [xt, nm], [xb], scale=nm[0:nt, 2:3])
        ps = K.psget()
        psb = ps.t[:].bitcast(BF16)
        for kc in range(8):
            K.tr_pe(psb[:, kc * 128:kc * 128 + nt], xb[0:nt, kc * 128:(kc + 1) * 128], ident_b[0:nt, 0:nt],
                    [xb, ident_b], [ps], inc=(kc == 7))
        psv = psb.rearrange("p (a b) -> p a b", b=128)
        K.tt("dve", hT[:, :, t * nt:(t + 1) * nt], psv[:, :, 0:nt], G[gi][:, :, 0:nt], ALU.mult, [ps, G[gi]], [hT])
        K.psput(ps)

    def silu_from_psum(out_ap, out_tile, ps, ps_ap, W):
        sg = rn[0]
        K.act(sg[:, 0:W], ps_ap, AF.Exp, [ps], [sg], scale=-1.0)
        K.act(sg[:, 0:W], sg[:, 0:W], AF.Ln, [sg], [sg], bias=1.0)
        K.act(sg[:, 0:W], sg[:, 0:W], AF.Exp, [sg], [sg], scale=-1.0)
        K.tt("dve", out_ap, ps_ap, sg[:, 0:W], ALU.mult, [ps, sg], [out_tile])

    def rsqrt_mean(ps_ap, W, r_tiles, key):
        a = rot(rn, key + "rn")
        b = rot(rinv, key + "ri")
        K.act(a[:, 0:W], ps_ap, AF.Ln, r_tiles, [a], bias=EPS)
        K.act(b[:, 0:W], a[:, 0:W], AF.Exp, [a], [b], scale=-0.5)
        return b

    def setup_mem_from_tokens(kst_tile_list):
        for kt, (kap, vap, trs) in enumerate(kst_tile_list):
            K.copy("pool", mv_b[:, kt, :], vap, trs, [mv_b])
            K.copy("dve", mk_b[:, :], kap, trs, [mk_b])
            ps = K.psget()
            psb = ps.t[:].bitcast(BF16)
            for h in range(4):
                K.tr_pe(psb[:, h * 128:(h + 1) * 128], mk_b[:, h * 128:(h + 1) * 128], ident_b[:, :],
                        [mk_b, ident_b], [ps], inc=(h == 3))
            K.copy("act", mkT[:, :, kt * 128:(kt + 1) * 128], psb[:, 0:512].rearrange("p (a b) -> p a b", b=128),
                   [ps], [mkT])
            K.psput(ps)

    def prompt_mem():
        for t in range(2):
            K.dma(xin[t][:], d_mem[t * 128:(t + 1) * 128, :], xin[t])
            rms_to_hT(xin[t], 128, t, 3)
        lst = []
        uTf = uT[:].rearrange("p a b -> p (a b)")
        for c in range(2):
            K.dma(uTf[:, :], s_wbf[NSTREAM + c], uT, s_wbf)
            wv = uTf[:, :].rearrange("p (a b) -> p a b", b=512)
            for t in range(2):
                ps = K.psget()
                for kc in range(8):
                    K.mm(ps[:, :], hT[:, kc, t * 128:(t + 1) * 128], wv[:, kc, :], [hT, uT], [ps],
                         start=(kc == 0), stop=(kc == 7))
                K.copy("act" if c == 0 else "dve", memst[t][:, c * 512:(c + 1) * 512], ps[:, :], [ps], [memst[t]])
                K.psput(ps)
        for t in range(2):
            K.dma(o_pmk[t * 128:(t + 1) * 128, :], memst[t][:, 0:512], o_pmk, memst[t])
            K.dma(o_pmv[t * 128:(t + 1) * 128, :], memst[t][:, 512:1024], o_pmv, memst[t])
            lst.append((memst[t][:, 0:512], memst[t][:, 512:1024], [memst[t]]))
        setup_mem_from_tokens(lst)

    def sample_mem(si):
        lst = []
        for t in range(2):
            K.dma(memst[t][:, 0:512], d_cmk[si, t * 128:(t + 1) * 128, :], memst[t])
            K.dma(memst[t][:, 512:1024], d_cmv[si, t * 128:(t + 1) * 128, :], memst[t])
            lst.append((memst[t][:, 0:512], memst[t][:, 512:1024], [memst[t]]))
        setup_mem_from_tokens(lst)

    def kv_to_scratch(kb_tile, kb_ap, vb_tile, vb_ap, nt, s_kT, s_v, kpos, ti):
        K.dma(s_v[kpos:kpos + nt, :], vb_ap, s_v, vb_tile)
        ps = K.psget()
        psb = ps.t[:].bitcast(BF16)
        for h in range(4):
            K.tr_pe(psb[:, h * 128:h * 128 + nt], kb_ap[:, h * 128:(h + 1) * 128], ident_b[0:nt, 0:nt],
                    [kb_tile, ident_b], [ps], inc=(h == 3))
        kst = kstage[ti]
        K.copy("act", kst[:, :, 0:nt], psb[:, 0:512].rearrange("p (a b) -> p a b", b=128)[:, :, 0:nt], [ps], [kst])
        K.psput(ps)
        K.dma(s_kT[:, :, kpos:kpos + nt].rearrange("h p t -> p h t"), kst[:, :, 0:nt], s_kT, kst)

    class Stop(Exception):
        pass

    def chk(n):
        if stage == n:
            raise Stop()

    def drive(g1, g2, n1, n2):
        d1 = d2 = 0
        a1 = a2 = True
        while a1 or a2:
            pick1 = a1 and ((not a2) or (d1 * n2 <= d2 * n1))
            if pick1:
                try:
                    next(g1)
                    d1 += 1
                except StopIteration:
                    a1 = False
            else:
                try:
                    next(g2)
                    d2 += 1
                except StopIteration:
                    a2 = False

    def run_seq(d_x, o_y, o_dk, o_dv, o_sg, o_sc, d_rope, s_kT, s_v, T, nt, NT, past, chunk, nlev,
                init_state_ap, init_conv_ap):
        W = nt * NT
        nST = T // W
        if init_state_ap is None:
            for h in range(4):
                K.memset("pool", S_f[h][:], 0.0, [S_f[h]])
                K.memset("pool", S_b[h][:], 0.0, [S_b[h]])
            K.memset("pool", xpre[:, :, 0:3], 0.0, [xpre])
        else:
            for h in range(4):
                K.dma(S_f[h][:], init_state_ap[h], S_f[h])
                K.copy("pool", S_b[h][:], S_f[h][:], [S_f[h]], [S_b[h]])
            for blk in range(12):
                K.dma(convst[:, blk, :], init_conv_ap[:, blk * 128:(blk + 1) * 128].rearrange("t p -> p t"),
                      convst, allow_slow_non_contiguous=True)
            K.copy("dve", xpre[:, :, 0:3], convst[:, :, :], [convst], [xpre])

        def make_st(st):
            tok0 = st * W
            last_st = (st == nST - 1)
            xc = xin2[st % 2]
            mixc = mixT2[st % 2]

            def front():
                for t in range(NT):
                    K.dma(xc[t][0:nt, :], d_x[tok0 + t * nt: tok0 + (t + 1) * nt, :], xc[t])
                    rms_to_hT(xc[t], nt, t, 0)
                chk(10)
                for c in range(4):
                    wt = ws_next(c)
                    wv = wt[:].rearrange("p (a b) -> p a b", b=512)
                    for j in range(4):
                        blk = c * 4 + j
                        ps = K.psget()
                        for kc in range(8):
                            K.mm(ps[:, 0:W], wv[:, kc, j * 128:(j + 1) * 128], hT[:, kc, 0:W], [wt, hT], [ps],
                                 start=(kc == 0), stop=(kc == 7))
                        if blk < 12:
                            K.copy("act" if blk % 2 == 0 else "dve", xpre[:, blk, 3:3 + W], ps[:, 0:W], [ps], [xpre])
                            if last_st:
                                K.copy("dve", convst[:, blk, :], ps[:, W - 3:W], [ps], [convst])
                        else:
                            silu_from_psum(zs[:, blk - 12, 0:W], zs, ps, ps[:, 0:W], W)
                        K.psput(ps)
                if last_st:
                    for blk in range(12):
                        K.dma(o_sc[0][:, blk * 128:(blk + 1) * 128].rearrange("t p -> p t"), convst[:, blk, :], o_sc[1], convst,
                              allow_slow_non_contiguous=True)
                chk(11)
                psA = K.psget()
                psB = K.psget()
                for kc in range(8):
                    K.mm(psA[0:4, 0:W], wab[:, kc, 0:4], hT[:, kc, 0:W], [wab, hT], [psA], start=(kc == 0), stop=(kc == 7))
                for kc in range(8):
                    K.mm(psB[0:4, 0:W], wab[:, kc, 4:8], hT[:, kc, 0:W], [wab, hT], [psB], start=(kc == 0), stop=(kc == 7))
                K.act(abT[:, 0, 0:W], psB[0:4, 0:W], AF.Exp, [psB], [abT], scale=-1.0)
                K.act(abT[:, 0, 0:W], abT[:, 0, 0:W], AF.Ln, [abT], [abT], bias=1.0)
                K.act(abT[:, 0, 0:W], abT[:, 0, 0:W], AF.Exp, [abT], [abT], scale=-1.0)
                K.act(abT[:, 1, 0:W], psA[0:4, 0:W], AF.Exp, [psA, smallp], [abT], bias=smallp[0:4, 3:4])
                K.act(abT[:, 1, 0:W], abT[:, 1, 0:W], AF.Ln, [abT], [abT], bias=1.0)
                K.ts("dve", abT[:, 1, 0:W], abT[:, 1, 0:W], derived[0:4, 2:3], ALU.mult, [abT, derived], [abT])
                K.psput(psA)
                K.psput(psB)
                for t in range(NT):
                    K.op("dve", lambda t=t: nc.vector.tensor_tensor_scan(abT[:, 2, t * nt:(t + 1) * nt], ones_f[:, 0:nt],
                                                                         abT[:, 1, t * nt:(t + 1) * nt], 0.0, ALU.mult, ALU.add),
                         [ones_f, abT], [abT])
                for h in range(4):
                    ps = K.psget()
                    K.mm(ps[:, 0:W], sel[0:4, h * 128:(h + 1) * 128], abT[:, 2, 0:W], [sel, abT], [ps])
                    K.mm(ps[:, 256:256 + W], sel[0:4, h * 128:(h + 1) * 128], abT[:, 0, 0:W], [sel, abT], [ps])
                    K.copy("dve", GCrow[:, h, 0:W], ps[:, 0:W], [ps], [GCrow])
                    K.act(EGrow[:, h, 0:W], ps[:, 0:W], AF.Exp, [ps], [EGrow])
                    K.copy("dve", BROW[:, h, 0:W], ps[:, 256:256 + W], [ps], [BROW])
                    K.psput(ps)
                for t in range(NT):
                    ps = K.psget()
                    K.mm(ps[0:nt, 0:4], abT[:, 2, t * nt:(t + 1) * nt], ident_f[0:4, 0:4], [abT, ident_f], [ps])
                    K.mm(ps[0:nt, 4:8], abT[:, 0, t * nt:(t + 1) * nt], ident_f[0:4, 0:4], [abT, ident_f], [ps])
                    cl = cols[t]
                    K.copy("dve", cl[0:nt, 0:8], ps[0:nt, 0:8], [ps], [cl])
                    K.psput(ps)
                    K.act(cl[0:nt, 8:12], cl[0:nt, 0:4], AF.Exp, [cl], [cl])
                    K.tt("dve", cl[0:nt, 12:16], cl[0:nt, 4:8], cl[0:nt, 8:12], ALU.mult, [cl], [cl])
                    lastc = t * nt + nt - 1
                    for h in range(4):
                        K.act(cl[0:nt, 16 + h:17 + h], cl[0:nt, h:h + 1], AF.Exp, [cl, GCrow], [cl], scale=-1.0,
                              bias=GCrow[0:nt, h, lastc:lastc + 1])
                chk(12)
                for ci, c in enumerate((4, 5, 6)):
                    wt = ws_next(c)
                    wv = wt[:].rearrange("p (a b) -> p a b", b=512)
                    for t in range(NT):
                        ps = K.psget()
                        for kc in range(8):
                            K.mm(ps[0:nt, :], hT[:, kc, t * nt:(t + 1) * nt], wv[:, kc, :], [hT, wt], [ps],
                                 start=(kc == 0), stop=(kc == 7))
                        if ci < 2:
                            K.copy("act", qk_st[t][0:nt, ci * 512:(ci + 1) * 512], ps[0:nt, :], [ps], [qk_st[t]])
                        else:
                            K.copy("dve", v_st[t][0:nt, :], ps[0:nt, :], [ps], [v_st[t]])
                        K.psput(ps)
                for t in range(NT):
                    qs = qk_st[t]
                    K.dma(cs[t][0:nt, :], d_rope[tok0 + t * nt: tok0 + (t + 1) * nt, :], cs[t])
                    xv = qs[0:nt, :].rearrange("p (g d) -> p g d", d=64)
                    x1 = xv[:, :, 0:8]
                    x2 = xv[:, :, 8:16]
                    cosv = cs[t][0:nt, 0:128].rearrange("p (g d) -> p g d", d=8)
                    sinv = cs[t][0:nt, 128:256].rearrange("p (g d) -> p g d", d=8)
                    rt = [ropet[0:nt, i, :].rearrange("p (g d) -> p g d", d=8) for i in range(4)]
                    K.tt("dve", rt[0], x1, cosv, ALU.mult, [qs, cs[t]], [ropet])
                    K.tt("dve", rt[1], x2, sinv, ALU.mult, [qs, cs[t]], [ropet])
                    K.tt("dve", rt[2], x2, cosv, ALU.mult, [qs, cs[t]], [ropet])
                    K.tt("dve", rt[3], x1, sinv, ALU.mult, [qs, cs[t]], [ropet])
                    K.tt("dve", x1, rt[0], rt[1], ALU.subtract, [ropet], [qs])
                    K.tt("dve", x2, rt[2], rt[3], ALU.add, [ropet], [qs])
                    K.dma(o_dk[0][tok0 + t * nt: tok0 + (t + 1) * nt, :], qs[0:nt, 512:1024], o_dk[1], qs)
                    K.dma(o_dv[0][tok0 + t * nt: tok0 + (t + 1) * nt, :], v_st[t][0:nt, :], o_dv[1], v_st[t])
                    K.copy("dve", qk_b[t][0:nt, :], qs[0:nt, :], [qs], [qk_b[t]])
                    K.copy("act", v_b[t][0:nt, :], v_st[t][0:nt, :], [v_st[t]], [v_b[t]])
                    kv_to_scratch(qk_b[t], qk_b[t][0:nt, 512:1024], v_b[t], v_b[t][0:nt, :], nt, s_kT, s_v,
                                  past + tok0 + t * nt, t)
                    ps = K.psget()
                    psb = ps.t[:].bitcast(BF16)
                    for h in range(4):
                        K.tr_pe(psb[:, h * 128:h * 128 + nt], qk_b[t][0:nt, h * 128:(h + 1) * 128], ident_b[0:nt, 0:nt],
                                [qk_b[t], ident_b], [ps], inc=(h == 3))
                    for s_ in range(2):
                        K.act(qdT[64 * s_:64 * s_ + 64, s_, :, t * nt:(t + 1) * nt],
                              psb[64 * s_:64 * s_ + 64, 0:512].rearrange("p (a b) -> p a b", b=128)[:, :, 0:nt],
                              AF.Copy, [ps], [qdT], scale=0.125)
                    K.psput(ps)
            def gdn_gen():
                chk(13)
                QK8 = list(range(8))
                ytj = {j: QP[TH[j]] for j in QK8}
                sgj = {j: RP[TH[j]] for j in QK8}
                sqj = {j: TTm[TH[j]] for j in QK8}
                fl = lambda tl: tl.t[:].rearrange("p a b -> p (a b)")
                for j in QK8:
                    ps = K.psget()
                    for tap in range(4):
                        K.mm(ps[:, 0:W], diagw[:, j * 4 + tap, :], xpre[:, j, tap:tap + W], [diagw, xpre], [ps],
                             start=(tap == 0), stop=(tap == 3))
                    K.copy("dve" if j % 2 == 0 else "act", fl(ytj[j])[:, 0:W], ps[:, 0:W], [ps], [ytj[j]])
                    K.psput(ps)
                    yield
                for j in QK8:
                    K.act(fl(sgj[j])[:, 0:W], fl(ytj[j])[:, 0:W], AF.Exp, [ytj[j]], [sgj[j]], scale=-1.0)
                for j in QK8:
                    K.act(fl(sgj[j])[:, 0:W], fl(sgj[j])[:, 0:W], AF.Ln, [sgj[j]], [sgj[j]], bias=1.0)
                yield
                for j in QK8:
                    K.act(fl(sgj[j])[:, 0:W], fl(sgj[j])[:, 0:W], AF.Exp, [sgj[j]], [sgj[j]], scale=-1.0)
                for j in QK8:
                    K.tt("dve", fl(ytj[j])[:, 0:W], fl(ytj[j])[:, 0:W], fl(sgj[j])[:, 0:W], ALU.mult, [ytj[j], sgj[j]], [ytj[j]])
                yield
                for j in QK8:
                    sqv = sqj[j].t[:].bitcast(BF16)
                    K.tt("pool", sqv[:, 0:W], fl(ytj[j])[:, 0:W], fl(ytj[j])[:, 0:W], ALU.mult, [ytj[j]], [sqj[j]])
                for j in QK8:
                    sqv = sqj[j].t[:].bitcast(BF16)
                    ps2 = K.psget()
                    K.mm(ps2[:, 0:W], ones_b[:, :], sqv[:, 0:W], [ones_b, sqj[j]], [ps2])
                    K.act(fl(sgj[j])[:, 0:W], ps2[:, 0:W], AF.Ln, [ps2], [sgj[j]], bias=EPS)
                    K.psput(ps2)
                    yield
                for j in QK8:
                    K.act(fl(sgj[j])[:, 0:W], fl(sgj[j])[:, 0:W], AF.Exp, [sgj[j]], [sgj[j]], scale=-0.5)
                for j in QK8:
                    h = j % 4
                    if j < 4:
                        K.stt(qnT[:, h, 0:W], fl(ytj[j])[:, 0:W], float(128 ** -0.5), fl(sgj[j])[:, 0:W], ALU.mult, ALU.mult,
                              [ytj[j], sgj[j]], [qnT])
                        K.tt("pool", qgT[:, h, 0:W], qnT[:, h, 0:W], EGrow[:, h, 0:W], ALU.mult, [qnT, EGrow], [qgT])
                    else:
                        K.tt("dve", knT[:, h, 0:W], fl(ytj[j])[:, 0:W], fl(sgj[j])[:, 0:W], ALU.mult, [ytj[j], sgj[j]], [knT])
                yield
                for blk in range(8, 12):
                    ps = K.psget()
                    for tap in range(4):
                        K.mm(ps[:, 0:W], diagw[:, blk * 4 + tap, :], xpre[:, blk, tap:tap + W], [diagw, xpre], [ps],
                             start=(tap == 0), stop=(tap == 3))
                    silu_from_psum(vT[:, blk % 4, 0:W], vT, ps, ps[:, 0:W], W)
                    K.psput(ps)
                    yield
                K.copy("pool", xpre[:, :, 0:3], xpre[:, :, W:W + 3], [xpre], [xpre])
                chk(14)
                ITEMS = [(t, h) for t in range(NT) for h in range(4)]
                for (t, h) in ITEMS:
                    th = (t, h)
                    cl = cols[t]
                    sl = slice(t * nt, (t + 1) * nt)
                    ps = K.psget()
                    psb = ps.t[:].bitcast(BF16)
                    K.tr_pe(psb[0:nt, 0:128], knT[:, h, sl], ident_b[:, :], [knT, ident_b], [ps], inc=False)
                    K.tr_pe(psb[0:nt, 128:256], vT[:, h, sl], ident_b[:, :], [vT, ident_b], [ps], inc=True)
                    K.act(kd[th][0:nt, :], psb[0:nt, 0:128], AF.Copy, [ps, cl], [kd[th]], scale=cl[0:nt, 16 + h:17 + h])
                    K.act(kbg[th][0:nt, :], psb[0:nt, 0:128], AF.Copy, [ps, cl], [kbg[th]], scale=cl[0:nt, 12 + h:13 + h])
                    K.act(vb[th][0:nt, :], psb[0:nt, 128:256], AF.Copy, [ps, cl], [vb[th]], scale=cl[0:nt, 4 + h:5 + h])
                    K.psput(ps)
                    yield
                for (t, h) in ITEMS:
                    th = (t, h)
                    cl = cols[t]
                    sl = slice(t * nt, (t + 1) * nt)
                    dtm, dec, nb2 = Qm[th][1], Rm[th][1], TTm[th]
                    K.stt(dtm[0:nt, 0:nt], GCrow[0:nt, h, sl], cl[0:nt, h:h + 1], maskneg[0:nt, 0:nt], ALU.subtract, ALU.add,
                          [GCrow, cl, maskneg], [dtm])
                    K.act(dec[0:nt, 0:nt], dtm[0:nt, 0:nt], AF.Exp, [dtm], [dec])
                    K.tt("pool", dtm[0:nt, 0:nt], dec[0:nt, 0:nt], BROW[0:nt, h, sl], ALU.mult, [dec, BROW], [dtm])
                    K.tt("pool", nb2[0:nt, 0:nt], dtm[0:nt, 0:nt], noti[0:nt, 0:nt], ALU.mult, [dtm, noti], [nb2])
                yield
                for (t, h) in ITEMS:
                    th = (t, h)
                    sl = slice(t * nt, (t + 1) * nt)
                    dec, nb2 = Rm[th][1], TTm[th]
                    ps = K.psget()
                    K.mm(ps[0:nt, 0:nt], knT[:, h, sl], knT[:, h, sl], [knT], [ps])
                    K.mm(ps[0:nt, 128:128 + nt], knT[:, h, sl], qnT[:, h, sl], [knT, qnT], [ps])
                    K.tt("dve", aqkT[th][0:nt, 0:nt], ps[0:nt, 128:128 + nt], dec[0:nt, 0:nt], ALU.mult, [ps, dec], [aqkT[th]])
                    K.stt(Qm[th][0][0:nt, 0:nt], ps[0:nt, 0:nt], -1.0, nb2[0:nt, 0:nt], ALU.mult, ALU.mult, [ps, nb2], [Qm[th][0]])
                    K.psput(ps)
                    yield
                for (t, h) in ITEMS:
                    th = (t, h)
                    ps = K.psget()
                    K.tr_pe(ps[0:nt, 0:nt], Qm[th][0][0:nt, 0:nt], ident_f[0:nt, 0:nt], [Qm[th][0], ident_f], [ps])
                    K.copy("act", Rm[th][0][0:nt, 0:nt], ps[0:nt, 0:nt], [ps], [Rm[th][0]])
                    K.psput(ps)
                    yield
                    K.tt("pool", TTm[th][0:nt, 0:nt], Qm[th][0][0:nt, 0:nt], ident_f[0:nt, 0:nt], ALU.add,
                         [Qm[th][0], ident_f], [TTm[th]])
                chk(15)
                for lev in range(1, nlev + 1):
                    a = (lev - 1) % 2
                    b = lev % 2
                    lastlev = (lev == nlev)
                    for t in range(NT):
                        for h in range(4):
                            th = (t, h)
                            Q0, R0, Q1, R1 = Qm[th][a], Rm[th][a], Qm[th][b], Rm[th][b]
                            ps = K.psget()
                            if not lastlev:
                                K.mm(ps[0:nt, 0:nt], R0[0:nt, 0:nt], Q0[0:nt, 0:nt], [R0, Q0], [ps])
                            K.mm(ps[0:nt, 128:128 + nt], Q0[0:nt, 0:nt], R0[0:nt, 0:nt], [R0, Q0], [ps])
                            if not lastlev:
                                K.copy("act", Q1[0:nt, 0:nt], ps[0:nt, 0:nt], [ps], [Q1])
                            K.copy("dve", R1[0:nt, 0:nt], ps[0:nt, 128:128 + nt], [ps], [R1])
                            K.psput(ps)
                            yield
                    for t in range(NT):
                        for h in range(4):
                            th = (t, h)
                            R1, TTt = Rm[th][b], TTm[th]
                            ps = K.psget()
                            K.mm(ps[0:nt, 0:nt], R1[0:nt, 0:nt], TTt[0:nt, 0:nt], [R1, TTt], [ps])
                            K.tt("dve", TTt[0:nt, 0:nt], ps[0:nt, 0:nt], TTt[0:nt, 0:nt], ALU.add, [ps, TTt], [TTt])
                            K.psput(ps)
                            yield
                chk(16)
                for t in range(NT):
                    sl = slice(t * nt, (t + 1) * nt)
                    lastc = t * nt + nt - 1
                    nwh = {h: Qm[(t, h)][0] for h in range(4)}
                    vnv = {h: RP[(t, h)].t[:].rearrange("p a b -> p (a b)").bitcast(BF16) for h in range(4)}
                    vnt = {h: RP[(t, h)] for h in range(4)}
                    for h in range(4):
                        th = (t, h)
                        ps = K.psget()
                        K.mm(ps[:, 0:nt], kbg[th][0:nt, :], TTm[th][0:nt, 0:nt], [kbg[th], TTm[th]], [ps])
                        K.act(nwh[h][:, 0:nt], ps[:, 0:nt], AF.Copy, [ps], [nwh[h]], scale=-1.0)
                        K.psput(ps)
                        yield
                    for h in range(4):
                        th = (t, h)
                        ps = K.psget()
                        K.mm(ps[0:nt, 0:128], TTm[th][0:nt, 0:nt], vb[th][0:nt, :], [TTm[th], vb[th]], [ps], start=True, stop=False)
                        K.mm(ps[0:nt, 0:128], nwh[h][:, 0:nt], S_f[h][:, :], [nwh[h], S_f[h]], [ps], start=False, stop=True)
                        K.copy("act", vnv[h][0:nt, 0:128], ps[0:nt, 0:128], [ps], [vnt[h]])
                        K.psput(ps)
                        yield
                    for h in range(4):
                        th = (t, h)
                        ps = K.psget()
                        K.mm(ps[:, 0:nt], S_b[h][:, :], qgT[:, h, sl], [S_b[h], qgT], [ps], start=True, stop=False)
                        K.mm(ps[:, 0:nt], vnv[h][0:nt, 0:128], aqkT[th][0:nt, 0:nt], [vnt[h], aqkT[th]], [ps], start=False, stop=True)
                        K.copy("dve", oT[:, h, sl], ps[:, 0:nt], [ps], [oT])
                        K.psput(ps)
                        yield
                    for h in range(4):
                        th = (t, h)
                        ps = K.psget()
                        K.mm(ps[:, 0:128], kd[th][0:nt, :], vnv[h][0:nt, 0:128], [kd[th], vnt[h]], [ps])
                        K.ts("pool", S_f[h][:, :], S_f[h][:, :], EGrow[:, h, lastc:lastc + 1], ALU.mult, [S_f[h], EGrow], [S_f[h]])
                        K.tt("dve", S_f[h][:, :], ps[:, 0:128], S_f[h][:, :], ALU.add, [ps, S_f[h]], [S_f[h]])
                        K.psput(ps)
                        yield
                        K.copy("pool", S_b[h][:, :], S_f[h][:, :], [S_f[h]], [S_b[h]])
                if last_st:
                    for h in range(4):
                        K.dma(o_sg[0][h], S_f[h][:, :], o_sg[1], S_f[h])
                chk(17)
                for h in range(4):
                    sq = rot(sqtmp, "sq")
                    K.act(sq[:, 0:W], oT[:, h, 0:W], AF.Square, [oT], [sq])
                    ps = K.psget()
                    K.mm(ps[:, 0:W], onesm_b[:, :], sq[:, 0:W], [onesm_b, sq], [ps])
                    ri = rsqrt_mean(ps[:, 0:W], W, [ps], "gn")
                    K.psput(ps)
                    yield
                    yt = rot(ytmp, "yt")
                    K.stt(yt[:, 0:W], oT[:, h, 0:W], smallp[:, 0:1], ri[:, 0:W], ALU.mult, ALU.mult, [oT, smallp, ri], [yt])
                    K.tt("pool", mixc[:, h, 0:W], yt[:, 0:W], zs[:, h, 0:W], ALU.mult, [yt, zs], [mixc])

                yield
            def att_gen():
                chk(18)
                nkeys = past + tok0 + W
                deferred = [None]
                for h in range(4):
                    acc = [K.psget() for _ in range(2)]
                    for s_ in range(2):
                        K.memset("pool", dacc[s_][:, :, 0:W], 0.0, [dacc[s_]])
                    ucnt = [0]
                    nun = [0]
                    k0 = 0
                    first = True
                    pend = []
                    LA = 2

                    def flush_one():
                        vt_u, a_u, kn_u, q0_u, pt_u, first_u, lastk_u = pend.pop(0)
                        for s_u in range(2):
                            K.mm(acc[s_u][:, q0_u:W], vt_u[0:kn_u, a_u, :], pt_u[0:kn_u, s_u, q0_u:W], [vt_u, pt_u], [acc[s_u]],
                                 start=first_u, stop=lastk_u, inc=True)
                        da = dacc[ucnt[0] % 2]
                        ucnt[0] += 1
                        K.tt("dve", da[0:kn_u, :, q0_u:W], da[0:kn_u, :, q0_u:W], pt_u[0:kn_u, :, q0_u:W], ALU.add,
                             [da, pt_u], [da])
                    while k0 < nkeys:
                        gn = min(512, nkeys - k0)
                        kt_t = rot(kts, "kts")
                        vt_t = rot(vts, "vts")
                        K.dma(kt_t[:, 0:gn], s_kT[h, :, k0:k0 + gn], kt_t, s_kT)
                        nfull = gn // 128
                        if nfull > 0:
                            K.dma(vt_t[:, 0:nfull, :],
                                  s_v[k0:k0 + nfull * 128, h * 128:(h + 1) * 128].rearrange("(a p) d -> p a d", p=128),
                                  vt_t, s_v)
                        rem = gn - nfull * 128
                        if rem > 0:
                            K.dma(vt_t[0:rem, nfull, :], s_v[k0 + nfull * 128:k0 + gn, h * 128:(h + 1) * 128], vt_t, s_v)
                        nti = (gn + 127) // 128
                        for a in range(nti):
                            kn = min(128, gn - a * 128)
                            kpos = k0 + a * 128
                            q0 = 0
                            diag = False
                            if past == 0:
                                rel = kpos - tok0
                                if rel >= 0:
                                    q0 = rel
                                    diag = True
                            lastk = (kpos + kn >= nkeys)
                            pss = K.psget()
                            pssv = pss[:, :].rearrange("p (s q) -> p s q", s=2)
                            for s in range(2):
                                K.mm(pssv[0:kn, s, q0:W], kt_t[:, a * 128:a * 128 + kn],
                                     qdT[:, s, h, q0:W], [kt_t, qdT], [pss])
                            pt = rot(pTt, "pt")
                            K.act(pt[0:kn, :, q0:W], pssv[0:kn, :, q0:W], AF.Exp, [pss], [pt])
                            K.psput(pss)
                            yield
                            if diag and kn > chunk:
                                K.memset("pool", pt[chunk:kn, :, q0:q0 + chunk], 0.0, [pt])
                            pend.append((vt_t, a, kn, q0, pt, first, lastk))
                            if len(pend) > LA:
                                flush_one()
                            first = False
                            nun[0] += 1
                            if nun[0] == 3 and deferred[0] is not None:
                                deferred[0]()
                                deferred[0] = None
                        k0 += gn
                    while pend:
                        flush_one()
                    K.tt("dve", dacc[0][:, :, 0:W], dacc[0][:, :, 0:W], dacc[1][:, :, 0:W], ALU.add, [dacc[0], dacc[1]], [dacc[0]])
                    psl = K.psget()
                    for s in range(2):
                        K.mm(psl[:, s * 256:s * 256 + W], ones_ff[:, :], dacc[0][:, s, 0:W], [ones_ff, dacc[0]], [psl])

                    def epi(h=h, acc=acc, psl=psl):
                        for s in range(2):
                            K.act(rl[s][:, 0:W], psl[:, s * 256:s * 256 + W], AF.Ln, [psl], [rl[s]])
                            K.act(rl[s][:, 0:W], rl[s][:, 0:W], AF.Exp, [rl[s]], [rl[s]], scale=-1.0)
                            K.tt("dve", av[s][:, 0:W], acc[s][:, 0:W], rl[s][:, 0:W], ALU.mult, [acc[s], rl[s]], [av[s]])
                        K.psput(psl)
                        for p_ in acc:
                            K.psput(p_)
                        K.stt(att[:, 0:W], av[1][:, 0:W], derived[:, 1:2], av[0][:, 0:W], ALU.mult, ALU.add,
                              [av[0], av[1], derived], [att])
                        sq = sqtmp[0]
                        K.act(sq[:, 0:W], att[:, 0:W], AF.Square, [att], [sq])
                        ps = K.psget()
                        K.mm(ps[:, 0:W], onesm_b[:, :], sq[:, 0:W], [onesm_b, sq], [ps])
                        K.act(rl[0][:, 0:W], ps[:, 0:W], AF.Ln, [ps], [rl[0]], bias=EPS)
                        K.act(rl[1][:, 0:W], rl[0][:, 0:W], AF.Exp, [rl[0]], [rl[1]], scale=-0.5)
                        K.psput(ps)
                        K.stt(mixc[:, 4 + h, 0:W], att[:, 0:W], derived[:, 0:1], rl[1][:, 0:W], ALU.mult, ALU.mult,
                              [att, derived, rl[1]], [mixc])

                    if deferred[0] is not None:
                        deferred[0]()
                    deferred[0] = epi
                    yield
                if deferred[0] is not None:
                    deferred[0]()
                    deferred[0] = None
                yield
            n_att_est = 16 * ((past + tok0 + W + 127) // 128) + 8
            def back_gen():
                chk(19)
                for c in range(2):
                    wt = ws_next(7 + c)
                    wv = wt[:].rearrange("p (a b) -> p a b", b=512)
                    for t in range(NT):
                        ps = K.psget()
                        for kc in range(8):
                            K.mm(ps[0:nt, :], mixc[:, kc, t * nt:(t + 1) * nt], wv[:, kc, :], [mixc, wt], [ps],
                                 start=(kc == 0), stop=(kc == 7))
                        K.tt("dve", xc[t][0:nt, c * 512:(c + 1) * 512], ps[0:nt, :], xc[t][0:nt, c * 512:(c + 1) * 512],
                             ALU.add, [ps, xc[t]], [xc[t]])
                        K.psput(ps)
                        yield
                chk(20)
                for t in range(NT):
                    rms_to_hT(xc[t], nt, t, 1)
                wt = ws_next(9)
                wv = wt[:].rearrange("p (a b) -> p a b", b=512)
                for h in range(4):
                    ps = K.psget()
                    for kc in range(8):
                        K.mm(ps[:, 0:W], wv[:, kc, h * 128:(h + 1) * 128], hT[:, kc, 0:W], [wt, hT], [ps],
                             start=(kc == 0), stop=(kc == 7))
                    K.act(qmT[:, h, 0:W], ps[:, 0:W], AF.Copy, [ps], [qmT], scale=float(128 ** -0.5))
                    K.psput(ps)
                    yield
                for h in range(4):
                    pso = K.psget()
                    psl = K.psget()
                    for kt in range(2):
                        pss = K.psget()
                        K.mm(pss[:, 0:W], mkT[:, h, kt * 128:(kt + 1) * 128], qmT[:, h, 0:W], [mkT, qmT], [pss])
                        pt = rot(pTt, "pt")
                        K.act(pt[:, 0, 0:W], pss[:, 0:W], AF.Exp, [pss], [pt])
                        K.psput(pss)
                        yield
                        K.mm(pso[:, 0:W], mv_b[:, kt, h * 128:(h + 1) * 128], pt[:, 0, 0:W], [mv_b, pt], [pso],
                             start=(kt == 0), stop=(kt == 1), inc=True)
                        K.mm(psl[:, 0:W], ones_b[:, :], pt[:, 0, 0:W], [ones_b, pt], [psl],
                             start=(kt == 0), stop=(kt == 1), inc=True)
                    K.act(rl[0][:, 0:W], psl[:, 0:W], AF.Ln, [psl], [rl[0]])
                    K.act(rl[0][:, 0:W], rl[0][:, 0:W], AF.Exp, [rl[0]], [rl[0]], scale=-1.0)
                    K.tt("dve", omT[:, h, 0:W], pso[:, 0:W], rl[0][:, 0:W], ALU.mult, [pso, rl[0]], [omT])
                    K.psput(pso)
                    yield
                    K.psput(psl)
                    yield
                wt = ws_next(10)
                wv = wt[:].rearrange("p (a b) -> p a b", b=1024)
                for c in range(2):
                    for t in range(NT):
                        ps = K.psget()
                        for kc in range(4):
                            K.mm(ps[0:nt, :], omT[:, kc, t * nt:(t + 1) * nt], wv[:, kc, c * 512:(c + 1) * 512], [omT, wt], [ps],
                                 start=(kc == 0), stop=(kc == 3))
                        K.tt("dve", xc[t][0:nt, c * 512:(c + 1) * 512], ps[0:nt, :], xc[t][0:nt, c * 512:(c + 1) * 512],
                             ALU.add, [ps, xc[t]], [xc[t]])
                        K.psput(ps)
                        yield
                chk(21)
                for t in range(NT):
                    rms_to_hT(xc[t], nt, t, 2)
                accd = {(t, c2): K.psget() for t in range(NT) for c2 in range(2)}
                for half in range(2):
                    for c in range(4):
                        wt = ws_next(11 + half * 8 + c)
                        wv = wt[:].rearrange("p (a b) -> p a b", b=512)
                        for j in range(4):
                            blk = c * 4 + j
                            ps = K.psget()
                            for kc in range(8):
                                K.mm(ps[:, 0:W], wv[:, kc, j * 128:(j + 1) * 128], hT[:, kc, 0:W], [wt, hT], [ps],
                                     start=(kc == 0), stop=(kc == 7))
                            rt_ = rot(rtmp, "yt")
                            K.act(rt_[:, 0:W], ps[:, 0:W], AF.Relu, [ps], [rt_])
                            K.psput(ps)
                            yield
                            K.tt("pool" if blk % 2 == 0 else "dve", uT[:, blk, 0:W], rt_[:, 0:W], rt_[:, 0:W], ALU.mult, [rt_], [uT])
                    for c in range(4):
                        wt = ws_next(11 + half * 8 + 4 + c)
                        wv = wt[:].rearrange("p (a b) -> p a b", b=1024)
                        for t in range(NT):
                            for c2 in range(2):
                                for kk in range(4):
                                    K.mm(accd[(t, c2)][0:nt, :], uT[:, c * 4 + kk, t * nt:(t + 1) * nt],
                                         wv[:, kk, c2 * 512:(c2 + 1) * 512], [uT, wt], [accd[(t, c2)]],
                                         start=(half == 0 and c == 0 and kk == 0), stop=(half == 1 and c == 3 and kk == 3),
                                         inc=(kk == 3))
                for t in range(NT):
                    for c2 in range(2):
                        ps = accd[(t, c2)]
                        K.tt("dve", xc[t][0:nt, c2 * 512:(c2 + 1) * 512], ps[0:nt, :], xc[t][0:nt, c2 * 512:(c2 + 1) * 512],
                             ALU.add, [ps, xc[t]], [xc[t]])
                        K.psput(ps)
                        yield
                chk(22)
                for t in range(NT):
                    xb = xn_b[t]
                    nm = nrm[t]
                    K.act(xb[0:nt, :], xc[t][0:nt, :], AF.Square, [xc[t]], [xb, nm], accum=nm[0:nt, 0:1])
                    K.act(nm[0:nt, 1:2], nm[0:nt, 0:1], AF.Ln, [nm], [nm], scale=1.0 / DM, bias=EPS)
                    K.act(nm[0:nt, 2:3], nm[0:nt, 1:2], AF.Exp, [nm], [nm], scale=-0.5)
                    K.stt(yout[t][0:nt, :], xc[t][0:nt, :], nm[0:nt, 2:3], gfin[0:nt, :], ALU.mult, ALU.mult,
                          [xc[t], nm, gfin], [yout[t]])
                    K.dma(o_y[0][tok0 + t * nt: tok0 + (t + 1) * nt, :], yout[t][0:nt, :], o_y[1], yout[t])


                yield
            return front, gdn_gen, att_gen, back_gen

        prev_back = None
        for st in range(nST):
            front, gdn_gen, att_gen, back_gen = make_st(st)
            front()
            if prev_back is None:
                for _ in gdn_gen():
                    pass
            else:
                drive(gdn_gen(), prev_back(), 190, 70)
            for _ in att_gen():
                pass
            prev_back = back_gen
        for _ in prev_back():
            pass

    n_st_total = NS * 1 + TP // 256
    seq = []
    for _s in range(NS):
        seq += list(range(NSTREAM))
    nstp = TP // 256
    seq += list(range(7))
    for _s in range(nstp - 1):
        seq += list(range(7)) + list(range(7, NSTREAM))
    seq += list(range(7, NSTREAM))
    ws["seq"] = seq
    ws["total"] = len(seq)

    def _program():
        for si in range(NS):
            for kt in range(PAST // 128):
                t = kt % 2
                K.dma(qk_st[t][:, 512:1024], d_cdk[si, kt * 128:(kt + 1) * 128, :], qk_st[t])
                K.dma(v_st[t][:, :], d_cdv[si, kt * 128:(kt + 1) * 128, :], v_st[t])
                K.copy("act", qk_b[t][:, 512:1024], qk_st[t][:, 512:1024], [qk_st[t]], [qk_b[t]])
                K.copy("dve", v_b[t][:, :], v_st[t][:, :], [v_st[t]], [v_b[t]])
                kv_to_scratch(qk_b[t], qk_b[t][:, 512:1024], v_b[t], v_b[t][:, :], 128, s_kTs[si], s_vs[si], kt * 128, t)
            sample_mem(si)
            chk(3)
            run_seq(d_xs[si], (o_ys[si], o_ys), (o_sdk[si], o_sdk), (o_sdv[si], o_sdv), (o_ssg[si], o_ssg), (o_ssc[si], o_ssc),
                    d_ropeS, s_kTs[si], s_vs[si], TS_LEN, TS_LEN, 1, PAST, 64, 4, d_sg[si], d_sgc[si])
        chk(4)
        prompt_mem()
        chk(5)
        run_seq(d_xp, (o_yp[:], o_yp), (o_pdk[:], o_pdk), (o_pdv[:], o_pdv), (o_psg[:], o_psg), (o_psc[:], o_psc),
                d_ropeP, s_kTp, s_vp, TP, 128, 2, 0, 64, 6, None, None)

    try:
        _program()
    except Stop:
        pass
    K.finish()
    print("instructions:", K.n_ins, "sems:", K.nsem, {k: e.count for k, e in K.E.items()})
    return K

_CACHE = {}


def _chunks_from_weights(w_in, w_out, w_mq, w_mo, w_up, w_down, w_mkv):
    def colchunk(Wm, c0):
        blk = Wm[:, c0:c0 + 512]
        return blk.reshape(8, 128, 512).transpose(1, 0, 2).reshape(128, 4096)

    def rowchunk(Wm, r0):
        blk = Wm[r0:r0 + 512, :]
        return blk.reshape(4, 128, 1024).transpose(1, 0, 2).reshape(128, 4096)

    ch = []
    for c in range(4):
        ch.append(colchunk(w_in, c * 512))
    for c in range(3):
        ch.append(colchunk(w_in, 2056 + c * 512))
    for c in range(2):
        ch.append(colchunk(w_out, c * 512))
    ch.append(colchunk(w_mq, 0))
    ch.append(rowchunk(w_mo, 0))
    for half in range(2):
        for c in range(4):
            ch.append(colchunk(w_up, (half * 4 + c) * 512))
        for c in range(4):
            ch.append(rowchunk(w_down, (half * 4 + c) * 512))
    for c in range(2):
        ch.append(colchunk(w_mkv, c * 512))
    return np.ascontiguousarray(np.stack(ch, 0).astype(np.float32))


def _rope_table(pos):
    inv = (np.float32(500000.0) ** (-(np.arange(0, 16, 2, dtype=np.float32)) / np.float32(16))).astype(np.float32)
    ang = pos.astype(np.float32)[:, None] * inv[None, :]
    c = np.cos(ang).astype(np.float32)
    s = np.sin(ang).astype(np.float32)
    return np.ascontiguousarray(np.concatenate([np.tile(c, (1, 16)), np.tile(s, (1, 16))], axis=1).astype(np.float32))


def make_in_maps(inp, TP, cores, NS=2):
    f = lambda a: np.ascontiguousarray(np.asarray(a, dtype=np.float32))
    w_in = f(inp["w_in"])[0]
    wch = _chunks_from_weights(w_in, f(inp["w_out"])[0], f(inp["w_mq"])[0], f(inp["w_mo"])[0], f(inp["w_up"])[0],
                               f(inp["w_down"])[0], f(inp["w_mkv"])[0])
    wab = np.ascontiguousarray(w_in[:, 2048:2056].reshape(8, 128, 8).transpose(1, 0, 2).reshape(128, 64))
    gl = [f(inp[k])[0] for k in ("norm_mix_g", "norm_mem_g", "norm_ffn_g", "mem_norm_g")]
    gcols = np.ascontiguousarray(np.stack([g.reshape(8, 128).T for g in gl], 1).reshape(128, 32))
    gfin = np.ascontiguousarray(np.tile(f(inp["final_norm_g"])[None, :], (128, 1)))
    convw = np.ascontiguousarray(f(inp["gdn_conv_w"])[0].T.reshape(12, 128, 4).transpose(1, 0, 2).reshape(128, 48))
    small = np.zeros((128, 8), np.float32)
    small[:, 0] = f(inp["gdn_norm_g"])[0]
    small[:, 1] = f(inp["diff_norm_g"])[0]
    small[0:4, 2] = f(inp["gdn_a_log"])[0]
    small[0:4, 3] = f(inp["gdn_dt_bias"])[0]
    lamp = np.ascontiguousarray(f(inp["diff_lambda"])[0].reshape(1, 256))
    ident = np.eye(128, dtype=np.float32)
    r = np.arange(128)
    maskneg = np.where(r[None, :] >= r[:, None], 0.0, -1.0e5).astype(np.float32)
    noti = (1.0 - ident).astype(np.float32)
    selm = np.zeros((4, 512), np.float32)
    for h in range(4):
        selm[h, h * 128:(h + 1) * 128] = 1.0
    ropeP = _rope_table(np.arange(TP))
    ropeS = _rope_table(1024 + np.arange(32))
    xp = f(inp["x_prompt"]); xs = f(inp["x_sample"]); mem = f(inp["mem_prompt"])
    cdk = f(inp["cache_diff_k"])[0]; cdv = f(inp["cache_diff_v"])[0]
    cmk = f(inp["cache_mem_k"])[0]; cmv = f(inp["cache_mem_v"])[0]
    sg = f(inp["state_gdn"])[0]; sgc = f(inp["state_gdn_conv"])[0]
    maps = []
    for c in cores:
        sl = slice(c * NS, (c + 1) * NS)
        maps.append({
            "x_prompt": np.ascontiguousarray(xp[c, :TP]),
            "x_sample": np.ascontiguousarray(xs[sl]),
            "mem_prompt": np.ascontiguousarray(mem[c]),
            "cache_diff_k": np.ascontiguousarray(cdk[sl].reshape(NS, 1024, 512)),
            "cache_diff_v": np.ascontiguousarray(cdv[sl].reshape(NS, 1024, 512)),
            "cache_mem_k": np.ascontiguousarray(cmk[sl].reshape(NS, 256, 512)),
            "cache_mem_v": np.ascontiguousarray(cmv[sl].reshape(NS, 256, 512)),
            "state_gdn": np.ascontiguousarray(sg[sl]),
            "state_gdn_conv": np.ascontiguousarray(sgc[sl]),
            "wch": wch, "wab": wab, "gcols": gcols, "gfin": gfin, "convw": convw, "smallp": small, "lamp": lamp,
            "ident": ident, "maskneg": maskneg, "noti": noti, "sel": selm, "ropeP": ropeP, "ropeS": ropeS,
        })
    return maps


def kernel(**inputs):
    TP = 8192
    NCORE = 8
    if "K" not in _CACHE:
        _CACHE["K"] = build(TP)
    K = _CACHE["K"]
    maps = make_in_maps(inputs, TP, list(range(NCORE)))
    res = run_bass_kernel_spmd(K.nc, maps, core_ids=list(range(NCORE)))
    R = res.results
    g = lambda name: [np.asarray(r[name], dtype=np.float32) for r in R]
    y_prompt = np.stack(g("y_prompt"), 0)
    y_sample = np.concatenate(g("y_sample"), 0)
    p_state = np.stack(g("p_state_gdn"), 0)[None]
    p_conv = np.stack(g("p_state_gdn_conv"), 0)[None]
    p_dk = np.stack(g("p_diff_k"), 0).reshape(1, NCORE, TP, 4, 128)
    p_dv = np.stack(g("p_diff_v"), 0).reshape(1, NCORE, TP, 4, 128)
    p_mk = np.stack(g("p_mem_k"), 0).reshape(1, NCORE, 256, 4, 128)
    p_mv = np.stack(g("p_mem_v"), 0).reshape(1, NCORE, 256, 4, 128)
    s_state = np.concatenate(g("s_state_gdn"), 0)[None]
    s_conv = np.concatenate(g("s_state_gdn_conv"), 0)[None]
    s_dk = np.concatenate(g("s_diff_k"), 0).reshape(1, 2 * NCORE, 32, 4, 128)
    s_dv = np.concatenate(g("s_diff_v"), 0).reshape(1, 2 * NCORE, 32, 4, 128)
    return (y_prompt, y_sample, p_state, p_conv, p_dk, p_dv, p_mk, p_mv, s_state, s_conv, s_dk, s_dv)
```
